# Optimizing a Trainium2 kernel written in Bass

```python
import jax, jax.numpy as jnp
from jax import lax
import numpy as np

D_MODEL = 2048
BATCH = 4
SEQ = 4096
DEPTH = 1

MLA_HEADS = 8
QK_NOPE_DIM = 128
QK_ROPE_DIM = 64
V_HEAD_DIM = 128
Q_LORA_RANK = 512
KV_LORA_RANK = 256
ROPE_THETA = 10000.0
Q_BLOCK = 128
DIL_PATTERNS = ((128, 1), (512, 4), (2048, 16))
DIL_GROUPS = 3
DIL_HEADS_PER_GROUP = 4
DIL_HEADS = DIL_GROUPS * DIL_HEADS_PER_GROUP
DIL_HEAD_DIM = 128
DIL_BLOCK = 128
ALIBI_MAX_BIAS = 8.0
D_FF = 5504
CONV_WIDTH = 3
NORM_EPS = 1e-6

MLA_Q_DIM = MLA_HEADS * (QK_NOPE_DIM + QK_ROPE_DIM)
MLA_KV_DIM = MLA_HEADS * (QK_NOPE_DIM + V_HEAD_DIM)
DIL_QKV_DIM = DIL_HEADS * DIL_HEAD_DIM
DIL_OUT_DIM = DIL_HEADS_PER_GROUP * DIL_HEAD_DIM
IN_SPLITS = (Q_LORA_RANK, KV_LORA_RANK, QK_ROPE_DIM, DIL_QKV_DIM, DIL_QKV_DIM, DIL_QKV_DIM, D_MODEL, D_MODEL)
D_IN = Q_LORA_RANK + KV_LORA_RANK + QK_ROPE_DIM + 3 * DIL_QKV_DIM + 2 * D_MODEL

kernel_name = 'hybrid_mla_dilated_convffn'


def rmsnorm(x, g):
    xf = x.astype(jnp.float32)
    y = xf * lax.rsqrt(jnp.mean(xf * xf, axis=-1, keepdims=True) + NORM_EPS)
    return (y * g.astype(jnp.float32)).astype(x.dtype)


def rope(x, cos, sin):
    half = x.shape[-1] // 2
    xf = x.astype(jnp.float32)
    x1, x2 = xf[..., :half], xf[..., half:]
    return jnp.concatenate([x1 * cos - x2 * sin, x2 * cos + x1 * sin], axis=-1).astype(x.dtype)


def mla_attention(c_q, c_kv, k_pe_raw, q_norm_g, w_uq, kv_norm_g, w_ukv):
    B, S, _ = c_q.shape
    q = (rmsnorm(c_q, q_norm_g) @ w_uq).reshape(B, S, MLA_HEADS, QK_NOPE_DIM + QK_ROPE_DIM)
    kv = (rmsnorm(c_kv, kv_norm_g) @ w_ukv).reshape(B, S, MLA_HEADS, QK_NOPE_DIM + V_HEAD_DIM)
    q_nope, q_pe = q[..., :QK_NOPE_DIM], q[..., QK_NOPE_DIM:]
    k_nope, v = kv[..., :QK_NOPE_DIM], kv[..., QK_NOPE_DIM:]
    pos = jnp.arange(S, dtype=jnp.float32)
    inv_freq = ROPE_THETA ** (-jnp.arange(0, QK_ROPE_DIM, 2, dtype=jnp.float32) / QK_ROPE_DIM)
    ang = pos[:, None] * inv_freq[None, :]
    cos, sin = jnp.cos(ang), jnp.sin(ang)
    q_pe = rope(q_pe, cos[:, None, :], sin[:, None, :])
    k_pe = rope(k_pe_raw, cos, sin)
    scale = (QK_NOPE_DIM + QK_ROPE_DIM) ** -0.5
    nb = S // Q_BLOCK
    qn_b = q_nope.reshape(B, nb, Q_BLOCK, MLA_HEADS, QK_NOPE_DIM).transpose(1, 0, 2, 3, 4)
    qp_b = q_pe.reshape(B, nb, Q_BLOCK, MLA_HEADS, QK_ROPE_DIM).transpose(1, 0, 2, 3, 4)
    kpos = jnp.arange(S)

    def one_block(args):
        qn, qp, i = args
        s = (jnp.einsum('bqhd,bkhd->bhqk', qn, k_nope).astype(jnp.float32)
             + jnp.einsum('bqhr,bkr->bhqk', qp, k_pe).astype(jnp.float32)) * scale
        qpos = i * Q_BLOCK + jnp.arange(Q_BLOCK)
        s = jnp.where(kpos[None, :] <= qpos[:, None], s, -jnp.inf)
        p = jax.nn.softmax(s, axis=-1).astype(v.dtype)
        return jnp.einsum('bhqk,bkhd->bqhd', p, v)

    o = lax.map(one_block, (qn_b, qp_b, jnp.arange(nb)))
    return o.transpose(1, 0, 2, 3, 4).reshape(B, S, MLA_HEADS * V_HEAD_DIM)


def dilated_group(q, k, v, window, dil, slopes):
    B, S, H, D = q.shape
    w_sub = window // dil
    L = S // dil
    nb = -(-L // DIL_BLOCK)
    Lp = nb * DIL_BLOCK

    def to_blocks(t):
        t = t.reshape(B, L, dil, H, D).transpose(0, 2, 1, 3, 4)
        t = jnp.pad(t, ((0, 0), (0, 0), (0, Lp - L), (0, 0), (0, 0)))
        return t.reshape(B, dil, nb, DIL_BLOCK, H, D)

    def with_prev(t):
        prev = jnp.pad(t, ((0, 0), (0, 0), (1, 0), (0, 0), (0, 0), (0, 0)))[:, :, :-1]
        return jnp.concatenate([prev, t], axis=3)

    qb = to_blocks(q)
    kk = with_prev(to_blocks(k))
    vv = with_prev(to_blocks(v))
    s = jnp.einsum('brnqhd,brnkhd->brnhqk', qb, kk).astype(jnp.float32) * (D ** -0.5)
    p_idx = jnp.arange(DIL_BLOCK)
    k_idx = jnp.arange(2 * DIL_BLOCK)
    j = p_idx[:, None] + DIL_BLOCK - k_idx[None, :]
    valid = (j >= 0) & (j <= w_sub)
    first = jnp.arange(nb) == 0
    valid = valid[None] & ~(first[:, None, None] & (k_idx < DIL_BLOCK)[None, None, :])
    alibi = -slopes.astype(jnp.float32)[:, None, None] * (dil * j).astype(jnp.float32)[None]
    s = jnp.where(valid[None, None, :, None], s + alibi[None, None, None], -jnp.inf)
    lse = jax.nn.logsumexp(s, axis=-1)
    p = jnp.exp(s - lse[..., None]).astype(v.dtype)
    o = jnp.einsum('brnhqk,brnkhd->brnqhd', p, vv)

    def from_blocks(t):
        t = t.reshape((B, dil, Lp) + t.shape[4:])[:, :, :L]
        t = jnp.moveaxis(t, 1, 2)
        return t.reshape((B, S) + t.shape[3:])

    return from_blocks(o), from_blocks(lse.transpose(0, 1, 2, 4, 3))


def dilated_attention(dq, dk, dv):
    B, S, _ = dq.shape
    shp = (B, S, DIL_GROUPS, DIL_HEADS_PER_GROUP, DIL_HEAD_DIM)
    q, k, v = dq.reshape(shp), dk.reshape(shp), dv.reshape(shp)
    slopes = 2.0 ** (-ALIBI_MAX_BIAS * jnp.arange(1, DIL_HEADS + 1, dtype=jnp.float32) / DIL_HEADS)
    slopes = slopes.reshape(DIL_GROUPS, DIL_HEADS_PER_GROUP)
    outs, lses = [], []
    for g, (window, dil) in enumerate(DIL_PATTERNS):
        o_g, l_g = dilated_group(q[:, :, g], k[:, :, g], v[:, :, g], window, dil, slopes[g])
        outs.append(o_g)
        lses.append(l_g)
    o = jnp.stack(outs, axis=0)
    wts = jax.nn.softmax(jnp.stack(lses, axis=0), axis=0)
    out = jnp.sum(wts[..., None] * o.astype(jnp.float32), axis=0).astype(dq.dtype)
    return out.reshape(B, S, DIL_OUT_DIM)


def causal_dwconv(u, w, b):
    S = u.shape[1]
    upad = jnp.pad(u, ((0, 0), (CONV_WIDTH - 1, 0), (0, 0)))
    out = b
    for t in range(CONV_WIDTH):
        out = out + w[t] * upad[:, t:t + S]
    return out


def setup_inputs(seed: int = 0) -> dict:
    key = jax.random.key(seed)
    ks = jax.random.split(key, 17)

    def w(k, shape, fan_in):
        return jax.random.normal(k, shape, jnp.float32) * (fan_in ** -0.5)

    def gain(k, shape):
        return 1.0 + 0.02 * jax.random.normal(k, shape, jnp.float32)

    return {
        'x': jax.random.normal(ks[0], (BATCH, SEQ, D_MODEL), jnp.float32),
        'attn_norm_g': gain(ks[1], (DEPTH, D_MODEL)),
        'w_in': w(ks[2], (DEPTH, D_MODEL, D_IN), D_MODEL),
        'b_gate': 0.02 * jax.random.normal(ks[3], (DEPTH, 2 * D_MODEL), jnp.float32),
        'q_norm_g': gain(ks[4], (DEPTH, Q_LORA_RANK)),
        'w_uq': w(ks[5], (DEPTH, Q_LORA_RANK, MLA_Q_DIM), Q_LORA_RANK),
        'kv_norm_g': gain(ks[6], (DEPTH, KV_LORA_RANK)),
        'w_ukv': w(ks[7], (DEPTH, KV_LORA_RANK, MLA_KV_DIM), KV_LORA_RANK),
        'w_o_mla': w(ks[8], (DEPTH, MLA_HEADS * V_HEAD_DIM, D_MODEL), MLA_HEADS * V_HEAD_DIM),
        'w_o_dil': w(ks[9], (DEPTH, DIL_OUT_DIM, D_MODEL), DIL_OUT_DIM),
        'w_out': w(ks[10], (DEPTH, D_MODEL, D_MODEL), D_MODEL),
        'ffn_norm_g': gain(ks[11], (DEPTH, D_MODEL)),
        'w_up': w(ks[12], (DEPTH, D_MODEL, 2 * D_FF), D_MODEL),
        'conv_w': w(ks[13], (DEPTH, CONV_WIDTH, 2 * D_FF), CONV_WIDTH),
        'conv_b': 0.02 * jax.random.normal(ks[14], (DEPTH, 2 * D_FF), jnp.float32),
        'w_down': w(ks[15], (DEPTH, D_FF, D_MODEL), D_FF),
        'final_norm_g': gain(ks[16], (D_MODEL,)),
    }


def reference(x, attn_norm_g, w_in, b_gate, q_norm_g, w_uq, kv_norm_g, w_ukv, w_o_mla, w_o_dil,
              w_out, ffn_norm_g, w_up, conv_w, conv_b, w_down, final_norm_g):
    split_at = [int(c) for c in np.cumsum(IN_SPLITS)[:-1]]
    for l in range(DEPTH):
        h = rmsnorm(x, attn_norm_g[l])
        proj = h @ w_in[l]
        c_q, c_kv, k_pe, dq, dk, dv, ga, gb = jnp.split(proj, split_at, axis=-1)
        gate_a = jax.nn.sigmoid(ga + b_gate[l, :D_MODEL])
        gate_b = jax.nn.sigmoid(gb + b_gate[l, D_MODEL:])
        o_a = mla_attention(c_q, c_kv, k_pe, q_norm_g[l], w_uq[l], kv_norm_g[l], w_ukv[l]) @ w_o_mla[l]
        o_b = dilated_attention(dq, dk, dv) @ w_o_dil[l]
        x = x + (gate_a * o_a + gate_b * o_b) @ w_out[l]
        h2 = rmsnorm(x, ffn_norm_g[l])
        u = causal_dwconv(h2 @ w_up[l], conv_w[l], conv_b[l])
        up, gate = u[..., :D_FF], u[..., D_FF:]
        x = x + (jax.nn.silu(gate) * up) @ w_down[l]
    return rmsnorm(x, final_norm_g)
```

```python
import numpy as np
from contextlib import ExitStack
import concourse.bass as bass
import concourse.mybir as mybir
from concourse.bass_utils import run_bass_kernel_spmd

F32, BF16 = mybir.dt.float32, mybir.dt.bfloat16
AF = mybir.ActivationFunctionType
ALU = mybir.AluOpType

D = 2048
SEQ = 4096
NOWN = 2048
HALO0 = 2046
NEXT = NOWN + 2
D_FF = 5504
EPS = 1e-6
NEG = -30000.0
SC_MLA = 192.0 ** -0.5
SC_DIL = 128.0 ** -0.5
DILS = (1, 4, 16)

V_ATTN_G, V_FFN_G, V_FIN_G, V_Q_G, V_KV_G, V_BGATE, V_CONVB, V_CONVW, V_CTX, V_EPS, V_TINY, NVEC = \
    0, 16, 32, 48, 52, 54, 86, 172, 430, 431, 432, 433


DBG = {}


class T:
    __slots__ = ("name", "t", "lw", "rd")

    def __init__(self, name, t):
        self.name, self.t, self.lw, self.rd = name, t, None, []


class View:
    def __init__(self, parent, ap):
        self.p, self.t = parent, ap
    lw = property(lambda s: s.p.lw, lambda s, v: setattr(s.p, "lw", v))
    rd = property(lambda s: s.p.rd, lambda s, v: setattr(s.p, "rd", v))


class Op:
    __slots__ = ("eng", "fn", "reads", "writes", "dma", "key", "deps", "need_inc", "sem", "val",
                 "idx", "phase", "xdeps", "par")

    def __init__(self, eng, fn, reads, writes, dma, key, xdeps=None):
        self.eng, self.fn, self.reads, self.writes, self.dma, self.key = eng, fn, reads, writes, dma, key
        self.deps, self.need_inc, self.sem, self.val, self.xdeps = [], False, None, 0, xdeps or []
        self.par = False


class Prog:
    ENGS = ("pe", "act", "dve", "pool", "sp")

    def __init__(self, nc, es):
        self.nc, self.es = nc, es
        self.eobj = {"pe": nc.tensor, "act": nc.scalar, "dve": nc.vector, "pool": nc.gpsimd, "sp": nc.sync}
        self.esem = {e: es.enter_context(nc.semaphore("sem_" + e)) for e in self.ENGS}
        self.ecnt = {e: 0 for e in self.ENGS}
        self.ksem, self.kcnt = {}, {}
        self.waited = {e: {} for e in self.ENGS}
        self.ops, self.nops, self.phase = [], 0, 0
        self.last = {e: None for e in self.ENGS}
        self.pending_dma = []
        self.stores = []
        self.stats = {e: 0 for e in self.ENGS}

    def op(self, eng, fn, reads=(), writes=(), dma=False, key=None, xdeps=None):
        o = Op(eng, fn, list(reads), list(writes), dma, key, xdeps)
        o.idx, o.phase = self.nops, self.phase
        self.nops += 1
        self.ops.append(o)
        return o

    def dma(self, eng, out_ap, in_ap, reads=(), writes=(), key=None, par=False, **kw):
        assert key is not None
        o = self.op(eng, lambda E: E.dma_start(out=out_ap, in_=in_ap, **kw), reads, writes, True, key)
        o.par = par
        return o

    def _sem_for_key(self, key):
        if key not in self.ksem:
            self.ksem[key] = self.es.enter_context(self.nc.semaphore("k_" + key))
            self.kcnt[key] = 0
        return self.ksem[key]

    def end_phase(self):
        ops = self.ops
        for o in ops:
            deps = set(o.xdeps)
            for t in o.reads:
                if t.lw is not None:
                    deps.add(t.lw)
                t.rd.append(o)
            for t in o.writes:
                if t.lw is not None:
                    deps.add(t.lw)
                for r in t.rd:
                    deps.add(r)
                t.rd = []
                t.lw = o
            deps.discard(o)
            keep, best = [], {}
            for d in deps:
                if d.phase < o.phase:
                    continue
                if d.dma:
                    if o.dma and o.par and d.key == o.key and d.eng == o.eng:
                        continue
                    keep.append(d)
                elif d.eng == "pe" and o.eng == "pe" and not o.dma:
                    continue
                else:
                    b = best.get(d.eng)
                    if b is None or d.idx > b.idx:
                        best[d.eng] = d
            keep.extend(best.values())
            o.deps = keep
            for d in keep:
                d.need_inc = True
            if o.dma:
                self.pending_dma.append(o)
            self.last[o.eng] = o
        bar_deps = [x for x in self.last.values() if x is not None and x.phase == self.phase and not x.dma]
        for d in bar_deps:
            d.need_inc = True
        bar_deps = bar_deps + [d for d in self.pending_dma if d.phase == self.phase]
        bars = []
        for e in self.ENGS:
            b = Op(e, None, [], [], False, None)
            b.idx, b.phase, b.deps = self.nops, self.phase, bar_deps
            self.nops += 1
            bars.append(b)
        for o in ops + bars:
            E = self.eobj[o.eng]
            w = self.waited[o.eng]
            need = {}
            for d in o.deps:
                sid = id(d.sem)
                if sid not in need or need[sid][1] < d.val:
                    need[sid] = (d.sem, d.val)
            for sid, (sem, val) in need.items():
                if w.get(sid, 0) < val:
                    E.wait_ge(sem, val)
                    w[sid] = val
            if o.fn is None:
                continue
            if o.dma:
                o.sem = self._sem_for_key(o.key)
                self.kcnt[o.key] += 16
                o.val = self.kcnt[o.key]
                ins = o.fn(E)
                ins.then_inc(o.sem, 16)
            else:
                ins = o.fn(E)
                if o.need_inc:
                    self.ecnt[o.eng] += 1
                    o.sem, o.val = self.esem[o.eng], self.ecnt[o.eng]
                    ins.then_inc(o.sem, 1)
            self.stats[o.eng] += 1
        self.ops = []
        self.pending_dma = []
        self.phase += 1


class Ring:
    def __init__(self, tiles):
        self.tiles, self.i = tiles, 0

    def next(self):
        t = self.tiles[self.i % len(self.tiles)]
        self.i += 1
        return t


def build_program(stop_after=99, dbg=False):
    nc = bass.Bass("TRN2", target_bir_lowering=False)
    es = ExitStack()
    P = Prog(nc, es)

    def dram(name, shape, dt, kind):
        return nc.dram_tensor(name, list(shape), dt, kind=kind).ap()

    xT = dram("xT", [16, 128, 16, 256], F32, "ExternalInput")
    vecs_d = dram("vecs", [128, NVEC], F32, "ExternalInput")
    cos_d = dram("cosT", [64, SEQ], F32, "ExternalInput")
    sin_d = dram("sinT", [64, SEQ], F32, "ExternalInput")
    tri_d = dram("tri", [128, 128], F32, "ExternalInput")
    bm_d = dram("biasmat", [128, 12 * 2 * 128], F32, "ExternalInput")
    ident_d = dram("ident", [128, 128], F32, "ExternalInput")
    w_lat_d = dram("w_lat", [7, 128, 16 * 128], F32, "ExternalInput")
    w_dil_d = dram("w_dil", [36, 128, 16 * 128], F32, "ExternalInput")
    w_gate_d = dram("w_gate", [32, 128, 16 * 128], F32, "ExternalInput")
    w_uq_d = dram("w_uq", [16, 128, 4 * 128], F32, "ExternalInput")
    w_ukv_d = dram("w_ukv", [16, 128, 2 * 128], F32, "ExternalInput")
    w_omla_d = dram("w_omla", [16, 128, 8 * 128], F32, "ExternalInput")
    w_odil_d = dram("w_odil", [16, 128, 4 * 128], F32, "ExternalInput")
    w_out_d = dram("w_out", [16, 128, 16 * 128], F32, "ExternalInput")
    w_up_d = dram("w_up", [86, 128, 16 * 128], F32, "ExternalInput")
    w_dn0_d = dram("w_dn0", [16, 128, 22 * 128], F32, "ExternalInput")
    w_dn1_d = dram("w_dn1", [16, 128, 21 * 128], F32, "ExternalInput")
    outT = dram("outT", [4, 128, 16, 512], F32, "ExternalOutput")
    Hs = dram("Hs", [16, 128, 16, 256], BF16, "Internal")
    MOs = dram("MOs", [12, 128, NEXT], BF16, "Internal")
    Hs_T = T("Hs", None)
    MOs_T = T("MOs", None)

    def sb(ctx, name, shape, dt):
        return T(name, ctx.enter_context(nc.sbuf_tensor(name, list(shape), dt)))

    banks = [T("bank%d" % i, es.enter_context(nc.psum_tensor("bank%d" % i, [128, 512], F32))) for i in range(7)]
    tbank_t = es.enter_context(nc.psum_tensor("tbank", [128, 1024], BF16))
    tbank = T("tbank", tbank_t)
    tps = Ring([View(tbank, tbank_t[:, i * 128:(i + 1) * 128]) for i in range(8)])

    vecs = sb(es, "vecs_sb", [128, NVEC], F32)
    ones = sb(es, "ones_sb", [128, 128], BF16)
    tri = sb(es, "tri_sb", [128, 128], BF16)
    ident = sb(es, "ident_sb", [128, 128], BF16)
    P.dma("sp", vecs.t[:], vecs_d, writes=[vecs], key="c_vecs")
    P.dma("pool", tri.t[:], tri_d, writes=[tri], key="c_tri")
    P.dma("pool", ident.t[:], ident_d, writes=[ident], key="c_ident")
    P.op("dve", lambda E: E.memset(ones.t[:], 1.0), writes=[ones])
    ones32 = sb(es, "ones32_sb", [128, 128], F32)
    P.op("dve", lambda E: E.memset(ones32.t[:], 1.0), writes=[ones32])

    def dump(name, tile, shape2):
        d = dram("dbg_" + name, shape2, F32, "ExternalOutput")
        src = tile.t[:]
        if len(src.shape) == 3:
            d = d.rearrange("p (a b) -> p a b", a=src.shape[1])
        elif len(src.shape) == 4:
            d = d.rearrange("p (a b c) -> p a b c", a=src.shape[1], b=src.shape[2])
        P.dma("pool", d, src, reads=[tile], key="dbg_" + name)

    def sl(start, count, step):
        return slice(start, start + (count - 1) * step + 1, step)

    def vcol(c, n=128):
        return vecs.t[0:n, c:c + 1]

    def mm(out_T, out_ap, lhsT_ap, rhs_ap, start, stop, reads):
        P.op("pe", lambda E: E.matmul(out_ap, lhsT=lhsT_ap, rhs=rhs_ap, start=start, stop=stop),
             reads=reads, writes=[out_T])

    def act(out_T, out_ap, in_ap, func, reads, bias=None, scale=None):
        kw = {}
        if bias is not None:
            kw["bias"] = bias
        if scale is not None:
            kw["scale"] = scale
        P.op("act", lambda E: E.activation(out=out_ap, in_=in_ap, func=func, **kw), reads=reads, writes=[out_T])

    def tt(out_T, out_ap, in0, in1, op, reads, eng="dve"):
        P.op(eng, lambda E: E.tensor_tensor(out=out_ap, in0=in0, in1=in1, op=op), reads=reads, writes=[out_T])

    def stt(out_T, out_ap, in0, scalar, in1, op0, op1, reads):
        P.op("dve", lambda E: E.scalar_tensor_tensor(out=out_ap, in0=in0, scalar=scalar, in1=in1, op0=op0, op1=op1),
             reads=reads, writes=[out_T])

    def ts(out_T, out_ap, in0, s1, s2, op0, op1, reads):
        if s2 is None:
            P.op("dve", lambda E: E.tensor_scalar(out=out_ap, in0=in0, scalar1=s1, scalar2=None, op0=op0),
                 reads=reads, writes=[out_T])
        else:
            P.op("dve", lambda E: E.tensor_scalar(out=out_ap, in0=in0, scalar1=s1, scalar2=s2, op0=op0, op1=op1),
                 reads=reads, writes=[out_T])

    def recip(out_T, ap, reads):
        P.op("dve", lambda E: E.reciprocal(out=ap, in_=ap), reads=reads, writes=[out_T])

    def rstd_from_ss(ss_T, n, rs_T, inv_dim):
        act(rs_T, rs_T.t[:, 0:n], ss_T.t[:, 0:n], AF.Ln, [ss_T, vecs], bias=vcol(V_EPS), scale=inv_dim)
        act(rs_T, rs_T.t[:, 0:n], rs_T.t[:, 0:n], AF.Exp, [rs_T], scale=-0.5)

    TW = 256
    lat = ExitStack()
    cqn = sb(lat, "cqn", [128, 4, NEXT], BF16)
    ckvn = sb(lat, "ckvn", [128, 2, SEQ], BF16)
    kpe = sb(lat, "kpe", [64, SEQ], BF16)
    with ExitStack() as ph:
        wl = [sb(ph, "wl%d" % i, [128, 16, 128], BF16) for i in range(7)]
        for i in range(7):
            P.dma("pool", wl[i].t[:], w_lat_d[i].rearrange("p (k m) -> p k m", m=128), writes=[wl[i]], key="wl%d" % i)
        xr = Ring([sb(ph, "x1_%d" % i, [128, 16, TW], F32) for i in range(3)])
        sqr = Ring([sb(ph, "sq1_%d" % i, [128, 16, TW], BF16) for i in range(2)])
        hr = Ring([sb(ph, "h1_%d" % i, [128, 16, TW], BF16) for i in range(4)])
        rsr = Ring([sb(ph, "rs1_%d" % i, [128, TW], F32) for i in range(5)])
        raw = Ring([sb(ph, "raw1_%d" % i, [128, 4, TW], F32) for i in range(2)])
        sqs = Ring([sb(ph, "sqs1_%d" % i, [128, 4, TW], BF16) for i in range(2)])
        csr = Ring([sb(ph, "cs1_%d" % i, [64, 2, TW], F32) for i in range(4)])
        tmr = Ring([sb(ph, "tm1_%d" % i, [64, 2, TW], F32) for i in range(2)])
        pb = Ring(banks[0:5])
        ssb = Ring(banks[5:7])
        def stage_n(j):
            t0 = j * TW
            jt, jo = t0 // 512, t0 % 512
            xt = xr.next()
            P.dma("sp", xt.t[:], xT[j], writes=[xt], key="x1_%d" % (j % 3))
            cs = csr.next()
            P.dma("sp", cs.t[:, 0, :], cos_d[:, t0:t0 + TW], writes=[cs], key="cs1a_%d" % (j % 4))
            P.dma("sp", cs.t[:, 1, :], sin_d[:, t0:t0 + TW], writes=[cs], key="cs1b_%d" % (j % 4))
            sq = sqr.next()
            act(sq, sq.t[:], xt.t[:], AF.Square, [xt])
            ss = ssb.next()
            for kc in range(16):
                mm(ss, ss.t[:, 0:TW], ones.t[:], sq.t[:, kc, :], kc == 0, kc == 15, [ones, sq])
            rs = rsr.next()
            rstd_from_ss(ss, TW, rs, 1.0 / D)
            ht = hr.next()
            for kc in range(16):
                stt(ht, ht.t[:, kc, :], xt.t[:, kc, :], vcol(V_ATTN_G + kc), rs.t[:, 0:TW], ALU.mult, ALU.mult, [xt, rs, vecs])
            P.dma("pool", Hs[j], ht.t[:], reads=[ht], key="h1_%d" % (j % 4))
            return ht, cs

        def stage_l(j, ht, cs):
            t0 = j * TW

            def latent(chunks, gcol, nfeat, dst, dst_col, c0, n):
                rw, sqq = raw.next(), sqs.next()
                for ci, ch in enumerate(chunks):
                    b = pb.next()
                    for kc in range(16):
                        mm(b, b.t[:, 0:n], wl[ch].t[:, kc, :], ht.t[:, kc, c0:c0 + n], kc == 0, kc == 15, [wl[ch], ht])
                    act(rw, rw.t[:, ci, 0:n], b.t[:, 0:n], AF.Copy, [b])
                    act(sqq, sqq.t[:, ci, 0:n], b.t[:, 0:n], AF.Square, [b])
                s2 = ssb.next()
                for ci in range(len(chunks)):
                    mm(s2, s2.t[:, 0:n], ones.t[:], sqq.t[:, ci, 0:n], ci == 0, ci == len(chunks) - 1, [ones, sqq])
                r2 = rsr.next()
                rstd_from_ss(s2, n, r2, 1.0 / nfeat)
                for ci in range(len(chunks)):
                    stt(dst, dst.t[:, ci, dst_col:dst_col + n], rw.t[:, ci, 0:n], vcol(gcol + ci), r2.t[:, 0:n],
                        ALU.mult, ALU.mult, [rw, r2, vecs])

            if t0 + TW > HALO0:
                c0 = max(HALO0 - t0, 0)
                latent([0, 1, 2, 3], V_Q_G, 512, cqn, t0 + c0 - HALO0, c0, TW - c0)
            latent([4, 5], V_KV_G, 256, ckvn, t0, 0, TW)
            ba, bb = pb.next(), pb.next()
            for kc in range(16):
                mm(ba, ba.t[0:64, 0:TW], wl[6].t[:, kc, 0:64], ht.t[:, kc, :], kc == 0, kc == 15, [wl[6], ht])
            for kc in range(16):
                mm(bb, bb.t[0:64, 0:TW], wl[6].t[:, kc, 64:128], ht.t[:, kc, :], kc == 0, kc == 15, [wl[6], ht])
            tm = tmr.next()
            tt(tm, tm.t[:, 0, :], ba.t[0:64, 0:TW], cs.t[:, 0, :], ALU.mult, [ba, cs])
            tt(tm, tm.t[:, 1, :], bb.t[0:64, 0:TW], cs.t[:, 1, :], ALU.mult, [bb, cs])
            tt(kpe, kpe.t[:, t0:t0 + TW], tm.t[:, 0, :], tm.t[:, 1, :], ALU.add, [tm])
        NT1 = SEQ // TW
        LOOK1 = 2
        staged = [stage_n(j) for j in range(LOOK1)]
        for j in range(NT1):
            if j + LOOK1 < NT1:
                staged.append(stage_n(j + LOOK1))
            stage_l(j, *staged.pop(0))
        if dbg and stop_after == 1:
            dump("cqn", cqn, [128, 4 * NEXT])
            dump("ckvn", ckvn, [128, 2 * SEQ])
            dump("kpe", kpe, [64, SEQ])
        P.end_phase()
    if stop_after == 1:
        return nc, es, P

    QG = [(HALO0, 2)] + [(NOWN + 512 * i, 512) for i in range(4)]
    with ExitStack() as ph:
        wuq_all = sb(ph, "wuq", [128, 16, 4, 128], BF16)
        wukv_all = sb(ph, "wukv", [128, 16, 2, 128], BF16)
        P.dma("pool", wukv_all.t[:], w_ukv_d.rearrange("o p (k m) -> p o k m", m=128), writes=[wukv_all], key="wukv")
        P.dma("pool", wuq_all.t[:], w_uq_d.rearrange("o p (k m) -> p o k m", m=128), writes=[wuq_all], key="wuq")

        wuq = [View(wuq_all, wuq_all.t[:, i]) for i in range(16)]
        wukv = [View(wukv_all, wukv_all.t[:, i]) for i in range(16)]
        cosq = sb(ph, "cosq", [64, NEXT], F32)
        sinq = sb(ph, "sinq", [64, NEXT], F32)
        P.dma("sp", cosq.t[:], cos_d[:, HALO0:SEQ], writes=[cosq], key="cosq")
        P.dma("sp", sinq.t[:], sin_d[:, HALO0:SEQ], writes=[sinq], key="sinq")
        khr = Ring([sb(ph, "kh%d" % i, [128, SEQ], BF16) for i in range(2)])
        vhr = Ring([sb(ph, "vh%d" % i, [128, SEQ], BF16) for i in range(2)])
        qnr = Ring([sb(ph, "qn%d" % i, [128, NEXT], BF16) for i in range(2)])
        qpr = Ring([sb(ph, "qp%d" % i, [64, NEXT], BF16) for i in range(2)])
        ptr = Ring([sb(ph, "pt%d" % i, [128, 512], BF16) for i in range(8)])
        tmr = Ring([sb(ph, "tm2_%d" % i, [64, 2, 512], F32) for i in range(2)])
        rzr = Ring([sb(ph, "rz%d" % i, [128, 512], F32) for i in range(2)])
        osr = Ring([sb(ph, "os%d" % i, [128, 512], BF16) for i in range(2)])
        pb = Ring(banks[0:3])
        ob = Ring([(banks[3], banks[4]), (banks[5], banks[6])])
        pend, LOOK = [], 6
        for h in range(8):
            wk, wv, wqn, wqp = wukv[2 * h], wukv[2 * h + 1], wuq[2 * h], wuq[2 * h + 1]
            Kh, Vh, qn, qp = khr.next(), vhr.next(), qnr.next(), qpr.next()
            for tg in range(8):
                b = pb.next()
                for kc in range(2):
                    mm(b, b.t[:, :], wk.t[:, kc, :], ckvn.t[:, kc, tg * 512:(tg + 1) * 512], kc == 0, kc == 1, [wk, ckvn])
                act(Kh, Kh.t[:, tg * 512:(tg + 1) * 512], b.t[:, :], AF.Copy, [b])
            for tg in range(8):
                b = pb.next()
                for i in range(4):
                    tb = tg * 4 + i
                    for kc in range(2):
                        mm(b, b.t[:, i * 128:(i + 1) * 128], ckvn.t[:, kc, tb * 128:(tb + 1) * 128], wv.t[:, kc, :],
                           kc == 0, kc == 1, [wv, ckvn])
                act(Vh, Vh.t[:, tg * 512:(tg + 1) * 512], b.t[:, :], AF.Copy, [b])
            for (q0, n) in QG:
                c0 = q0 - HALO0
                b = pb.next()
                for kc in range(4):
                    mm(b, b.t[:, 0:n], wqn.t[:, kc, :], cqn.t[:, kc, c0:c0 + n], kc == 0, kc == 3, [wqn, cqn])
                act(qn, qn.t[:, c0:c0 + n], b.t[:, 0:n], AF.Copy, [b])
                ba, bb = pb.next(), pb.next()
                for kc in range(4):
                    mm(ba, ba.t[0:64, 0:n], wqp.t[:, kc, 0:64], cqn.t[:, kc, c0:c0 + n], kc == 0, kc == 3, [wqp, cqn])
                for kc in range(4):
                    mm(bb, bb.t[0:64, 0:n], wqp.t[:, kc, 64:128], cqn.t[:, kc, c0:c0 + n], kc == 0, kc == 3, [wqp, cqn])
                tm = tmr.next()
                tt(tm, tm.t[:, 0, 0:n], ba.t[0:64, 0:n], cosq.t[:, c0:c0 + n], ALU.mult, [ba, cosq])
                tt(tm, tm.t[:, 1, 0:n], bb.t[0:64, 0:n], sinq.t[:, c0:c0 + n], ALU.mult, [bb, sinq])
                tt(qp, qp.t[:, c0:c0 + n], tm.t[:, 0, 0:n], tm.t[:, 1, 0:n], ALU.add, [tm])
            for (q0, n) in QG:
                kb_last = (q0 + n - 1) // 128
                Ob, Zb = ob.next()
                for kb in range(kb_last + 1):
                    k0 = kb * 128
                    qlo = max(q0, k0)
                    nn = q0 + n - qlo
                    cl = qlo - HALO0
                    S = pb.next()
                    mm(S, S.t[:, 0:nn], Kh.t[:, k0:k0 + 128], qn.t[:, cl:cl + nn], True, False, [Kh, qn])
                    mm(S, S.t[:, 0:nn], kpe.t[0:64, k0:k0 + 128], qp.t[0:64, cl:cl + nn], False, True, [kpe, qp])
                    Pt = ptr.next()
                    act(Pt, Pt.t[:, 0:nn], S.t[:, 0:nn], AF.Exp, [S, vecs],
                        bias=(vcol(V_CTX) if k0 < NOWN else None), scale=SC_MLA)
                    if qlo < k0 + 128:
                        m = min(q0 + n, k0 + 128) - qlo
                        off = qlo - k0
                        tt(Pt, Pt.t[:, 0:m], Pt.t[:, 0:m], tri.t[:, off:off + m], ALU.mult, [Pt, tri])

                    def stage_b(Ob=Ob, Zb=Zb, Vh=Vh, Pt=Pt, k0=k0, qlo=qlo, q0=q0, n=n, nn=nn, kb=kb, kb_last=kb_last, h=h):
                        mm(Ob, Ob.t[:, qlo - q0:n], Vh.t[:, k0:k0 + 128], Pt.t[:, 0:nn], kb == 0, kb == kb_last, [Vh, Pt])
                        mm(Zb, Zb.t[:, qlo - q0:n], ones.t[:], Pt.t[:, 0:nn], kb == 0, kb == kb_last, [ones, Pt])
                        if kb == kb_last:
                            rz, osb = rzr.next(), osr.next()
                            ts(rz, rz.t[:, 0:n], Zb.t[:, 0:n], vcol(V_TINY), None, ALU.add, ALU.bypass, [Zb, vecs])
                            recip(rz, rz.t[:, 0:n], [rz])
                            tt(osb, osb.t[:, 0:n], Ob.t[:, 0:n], rz.t[:, 0:n], ALU.mult, [Ob, rz])
                            P.dma("sp", MOs[h][:, q0 - HALO0:q0 - HALO0 + n], osb.t[:, 0:n], reads=[osb],
                                  key="os%d" % ((osr.i - 1) % 2))
                    pend.append(stage_b)
                    while len(pend) > LOOK:
                        pend.pop(0)()
        while pend:
            pend.pop(0)()
        P.end_phase()
    if stop_after == 2:
        with ExitStack() as ph:
            mo_dbg = sb(ph, "mo_dbg", [128, 12, NEXT], BF16)
            P.dma("sp", mo_dbg.t[:], MOs.rearrange("c p t -> p c t"), writes=[mo_dbg], key="modbg")
            dump("mo", mo_dbg, [128, 12 * NEXT])
            P.end_phase()
        return nc, es, P

    lat.close()
    with ExitStack() as ph:
        bm = sb(ph, "bm", [128, 12, 256], F32)
        hr = Ring([sb(ph, "h3_%d" % i, [128, 2, 16, 256], BF16) for i in range(2)])
        wr = Ring([sb(ph, "w3_%d" % i, [128, 16, 128], BF16) for i in range(12)])
        qT = [sb(ph, "dq%d" % g, [128, 2560], BF16) for g in range(3)]
        KBASE = (1536, 1024, 0)
        kT = [sb(ph, "dk%d" % g, [128, SEQ - KBASE[g]], BF16) for g in range(3)]
        vT = [sb(ph, "dv%d" % g, [128, SEQ - KBASE[g]], BF16) for g in range(3)]
        Uacc = sb(ph, "uacc", [128, NEXT], F32)
        Zacc = sb(ph, "zacc", [128, NEXT], F32)
        dout = sb(ph, "dout", [128, NEXT], BF16)
        vbr = Ring([sb(ph, "vb%d" % i, [128, 128], BF16) for i in range(24)])
        ptr = Ring([sb(ph, "pt3_%d" % i, [128, 256], BF16) for i in range(10)])
        tmr = Ring([sb(ph, "tm3_%d" % i, [128, 256], F32) for i in range(4)])
        pb = Ring(banks[0:4])
        sring = Ring(banks[0:3])
        ob = Ring([(banks[3], banks[4]), (banks[5], banks[6])])
        pend, LOOK = [], 6
        wcnt = [0]
        def prefetch(s):
            wt = {}
            for kind in range(3):
                for g in range(3):
                    w = wr.next()
                    mo = kind * 12 + g * 4 + s
                    P.dma("pool", w.t[:], w_dil_d[mo].rearrange("p (k m) -> p k m", m=128), writes=[w],
                          key="w3_%d" % (wcnt[0] % 12))
                    wcnt[0] += 1
                    wt[(kind, g)] = w
            hts = [load_h(0), load_h(1)]
            return wt, hts

        def load_h(j):
            ht = hr.next()
            P.dma("sp", ht.t[:], Hs[2 * j:2 * j + 2].rearrange("a p k t -> p a k t"), writes=[ht], key="h3_%d" % ((hr.i - 1) % 2))
            return ht

        pf = prefetch(0)
        P.dma("sp", bm.t[:], bm_d.rearrange("p (h c) -> p h c", h=12), writes=[bm], key="bm")
        for s in range(4):
            wt, hts = pf
            for j in range(8):
                t0 = j * 512
                need = [(kind, g) for g in range(3) for kind in (1, 2) if t0 >= KBASE[g]]
                if t0 >= 1536:
                    need += [(0, g) for g in range(3)]
                ht = hts[j] if j < 2 else load_h(j)
                for (kind, g) in need:
                    w = wt[(kind, g)]
                    b = pb.next()
                    for kc in range(16):
                        mm(b, b.t[:, :], w.t[:, kc, :], ht.t[:, :, kc, :], kc == 0, kc == 15, [w, ht])
                    dst, base = ((qT[g], 1536), (kT[g], KBASE[g]), (vT[g], KBASE[g]))[kind]
                    act(dst, dst.t[:, t0 - base:t0 - base + 512], b.t[:, :], AF.Copy, [b])
            for g in range(3):
                dil = DILS[g]
                hd = g * 4 + s
                units = []
                if g == 0:
                    units.append((0, 15, 126, 128, True))
                    units += [(0, n, 0, 128, True) for n in range(16, 32)]
                elif g == 1:
                    units += [(2, 3, 127, 128, True), (3, 3, 127, 128, True)]
                    units += [(r, n, 0, 128, True) for r in range(4) for n in range(4, 8)]
                else:
                    units += [(14, 0, 127, 128, False), (15, 0, 127, 128, False)]
                    units += [(r, 1, 0, 128, True) for r in range(16)]
                vcache = {}

                def vblock(r, n):
                    if (r, n) in vcache:
                        return vcache[(r, n)]
                    st = n * 128 * dil + r - KBASE[g]
                    tp = tps.next()
                    src = vT[g].t[:, sl(st, 128, dil)]
                    P.op("pe", lambda E: E.transpose(tp.t, src, ident.t[:]), reads=[vT[g], ident], writes=[tp])
                    vb = vbr.next()
                    P.op("act", lambda E: E.activation(out=vb.t[:], in_=tp.t, func=AF.Copy), reads=[tp], writes=[vb])
                    if len(vcache) >= 3:
                        vcache.pop(next(iter(vcache)))
                    vcache[(r, n)] = vb
                    return vb

                for (r, n, qa, qb, has_prev) in units:
                    nq = qb - qa
                    qtok = (n * 128 + qa) * dil + r
                    qs = qtok - 1536
                    q_ap = qT[g].t[:, sl(qs, nq, dil)]
                    blocks = [(n, 0)] + ([(n - 1, 1)] if has_prev else [])
                    S = sring.next()
                    for (kn, which) in blocks:
                        kst = kn * 128 * dil + r - KBASE[g]
                        mm(S, S.t[:, which * 128:which * 128 + nq], kT[g].t[:, sl(kst, 128, dil)], q_ap, True, True,
                           [kT[g], qT[g]])
                    tm, Pt = tmr.next(), ptr.next()
                    ctxs = [(kn * 128 + 127) * dil + r < NOWN for (kn, which) in blocks]
                    if nq == 128 and has_prev:
                        stt(tm, tm.t[:, 0:256], S.t[:, 0:256], SC_DIL, bm.t[:, hd, :], ALU.mult, ALU.add, [S, bm])
                    else:
                        for (kn, which) in blocks:
                            c = which * 128
                            stt(tm, tm.t[:, c:c + nq], S.t[:, c:c + nq], SC_DIL, bm.t[:, hd, c + qa:c + qb], ALU.mult, ALU.add,
                                [S, bm])
                    if nq == 128 and has_prev and ctxs[0] == ctxs[1]:
                        act(Pt, Pt.t[:, 0:256], tm.t[:, 0:256], AF.Exp, [tm, vecs], bias=(vcol(V_CTX) if ctxs[0] else None))
                    else:
                        for (kn, which), cx in zip(blocks, ctxs):
                            c = which * 128
                            act(Pt, Pt.t[:, c:c + nq], tm.t[:, c:c + nq], AF.Exp, [tm, vecs], bias=(vcol(V_CTX) if cx else None))
                    vbs = [(vblock(r, kn), which) for (kn, which) in blocks]
                    us = qtok - HALO0

                    def stage_b(vbs=vbs, Pt=Pt, nq=nq, us=us, g=g, dil=dil):
                        Ob, Zb = ob.next()
                        for i, (vb, which) in enumerate(vbs):
                            mm(Ob, Ob.t[:, 0:nq], vb.t[:], Pt.t[:, which * 128:which * 128 + nq], i == 0, i == len(vbs) - 1, [vb, Pt])
                        for i, (vb, which) in enumerate(vbs):
                            mm(Zb, Zb.t[:, 0:nq], ones.t[:], Pt.t[:, which * 128:which * 128 + nq], i == 0, i == len(vbs) - 1,
                               [ones, Pt])
                        u_ap = Uacc.t[:, sl(us, nq, dil)]
                        z_ap = Zacc.t[:, sl(us, nq, dil)]
                        if g == 0:
                            P.op("dve", lambda E, o=u_ap, i=Ob.t[:, 0:nq]: E.tensor_copy(out=o, in_=i), reads=[Ob], writes=[Uacc])
                            P.op("dve", lambda E, o=z_ap, i=Zb.t[:, 0:nq]: E.tensor_copy(out=o, in_=i), reads=[Zb], writes=[Zacc])
                        else:
                            tt(Uacc, u_ap, Ob.t[:, 0:nq], u_ap, ALU.add, [Ob, Uacc])
                            tt(Zacc, z_ap, Zb.t[:, 0:nq], z_ap, ALU.add, [Zb, Zacc])
                    pend.append(stage_b)
                    while len(pend) > LOOK:
                        pend.pop(0)()
            if s < 3:
                pf = prefetch(s + 1)
            while pend:
                pend.pop(0)()
            ts(Zacc, Zacc.t[:], Zacc.t[:], vcol(V_TINY), None, ALU.add, ALU.bypass, [Zacc, vecs])
            recip(Zacc, Zacc.t[:], [Zacc])
            tt(dout, dout.t[:], Uacc.t[:], Zacc.t[:], ALU.mult, [Uacc, Zacc])
            P.dma("sp", MOs[8 + s], dout.t[:], reads=[dout], key="dout")
        P.end_phase()
    if stop_after == 3:
        with ExitStack() as ph:
            mo_dbg = sb(ph, "mo_dbg", [128, 12, NEXT], BF16)
            P.dma("sp", mo_dbg.t[:], MOs.rearrange("c p t -> p c t"), writes=[mo_dbg], key="modbg")
            dump("mo", mo_dbg, [128, 12 * NEXT])
            P.end_phase()
        return nc, es, P

    with ExitStack() as ph:
        NC = 514
        xs = sb(ph, "x4", [128, 16, NC], F32)
        hs = sb(ph, "h4", [128, 16, NC], BF16)
        hsc = [T("h4c%d" % k, None) for k in range(16)]
        sqm = sb(ph, "sqm4", [128, 16, NC], BF16)
        mo = sb(ph, "mo4", [128, 12, NC], BF16)
        hid = sb(ph, "hid4", [128, 22, 512], BF16)
        hidb = sb(ph, "hidb4", [128, 2, 512], BF16)
        wr = Ring([sb(ph, "w4_%d" % i, [128, 16, 128], BF16) for i in range(8)])
        wdr = Ring([sb(ph, "wd4_%d" % i, [128, 22, 128], BF16) for i in range(3)])
        sgr = Ring([sb(ph, "sg4_%d" % i, [128, NC], BF16) for i in range(4)])
        tmr = Ring([sb(ph, "tm4_%d" % i, [128, NC], F32) for i in range(4)])
        u0r = Ring([sb(ph, "u04_%d" % i, [128, NC], F32) for i in range(4)])
        acr = Ring([sb(ph, "ac4_%d" % i, [128, 512], F32) for i in range(4)])
        rsr = Ring([sb(ph, "rs4_%d" % i, [128, NC], F32) for i in range(2)])
        carry = sb(ph, "carry4", [128, 86, 2], F32)
        pb = Ring(banks[0:5])
        wc = [0, 0]

        def wload(dram_ap, kcn):
            w = wr.next()
            P.dma("pool", w.t[:, 0:kcn, :], dram_ap.rearrange("p (k m) -> p k m", m=128), writes=[w],
                  key="w4_%d" % (wc[0] % 8))
            wc[0] += 1
            return w

        sqr = Ring([sb(ph, "sq4_%d" % i, [128, NC], BF16) for i in range(3)])
        ss_main, ss_halo = banks[5], banks[6]

        def load_h_mo(it):
            jt = (NOWN + it * 512) // 256
            P.dma("sp", hs.t[:, :, 2:258], Hs[jt], writes=hsc, key="h4", par=True)
            P.dma("sp", hs.t[:, :, 258:NC], Hs[jt + 1], writes=hsc, key="h4", par=True)
            if it == 0:
                P.dma("sp", hs.t[:, :, 0:2], Hs[7][:, :, 254:256], writes=hsc, key="h4", par=True)
            P.dma("sp", mo.t[:, :, 2:NC], MOs[:, :, 2 + it * 512:2 + (it + 1) * 512].rearrange("c p t -> p c t"),
                  writes=[mo], key="mo4")
            if it == 0:
                P.dma("sp", mo.t[:, :, 0:2], MOs[:, :, 0:2].rearrange("c p t -> p c t"), writes=[mo], key="mo4")

        def load_x(it):
            jt = (NOWN + it * 512) // 256
            P.dma("sp", xs.t[:, :, 2:258], xT[jt], writes=[xs], key="x4")
            P.dma("sp", xs.t[:, :, 258:NC], xT[jt + 1], writes=[xs], key="x4")
            if it == 0:
                P.dma("sp", xs.t[:, :, 0:2], xT[7][:, :, 254:256], writes=[xs], key="x4")

        load_h_mo(0)
        load_x(0)
        for it in range(4):
            segs = ([(0, 2)] if it == 0 else []) + [(2, 512)]
            lo = segs[0][0]

            def linear(w, kcn, rhs_T, rhs_of_kc):
                outs = []
                for (c, n) in segs:
                    b = pb.next()
                    for kc in range(kcn):
                        rT = rhs_T[kc] if isinstance(rhs_T, list) else rhs_T
                        mm(b, b.t[:, 0:n], w.t[:, kc, :], rhs_of_kc(kc, c, n), kc == 0, kc == kcn - 1, [w, rT])
                    outs.append((b, c, n))
                return outs

            def sumsq_then(m, lagq, seglist):
                sq = sqr.next()
                c_lo = seglist[0][0]
                act(sq, sq.t[:, c_lo:NC], xs.t[:, m, c_lo:NC], AF.Square, [xs])

                def emit(m=m, sq=sq):
                    for (c, n) in seglist:
                        sbk = ss_halo if n == 2 else ss_main
                        mm(sbk, sbk.t[:, 0:n], ones.t[:], sq.t[:, c:c + n], m == 0, m == 15, [ones, sq])
                lagq.append(emit)
                while len(lagq) > 1:
                    lagq.pop(0)()

            def rstd_cols(rs, seglist):
                for (c, n) in seglist:
                    sbk = ss_halo if n == 2 else ss_main
                    act(rs, rs.t[:, c:c + n], sbk.t[:, 0:n], AF.Ln, [sbk, vecs], bias=vcol(V_EPS), scale=1.0 / D)
                    act(rs, rs.t[:, c:c + n], rs.t[:, c:c + n], AF.Exp, [rs], scale=-0.5)

            for m in range(16):
                wga = wload(w_gate_d[m], 16)
                woa = wload(w_omla_d[m], 8)
                wgb = wload(w_gate_d[16 + m], 16)
                wob = wload(w_odil_d[m], 4)
                tms = tmr.next()
                for half, (wg, wo, kcn, mobase) in enumerate(((wga, woa, 8, 0), (wgb, wob, 4, 8))):
                    sg = sgr.next()
                    for (b, c, n) in linear(wg, 16, hsc, lambda kc, c, n: hs.t[:, kc, c:c + n]):
                        act(sg, sg.t[:, c:c + n], b.t[:, 0:n], AF.Sigmoid, [b, vecs], bias=vcol(V_BGATE + half * 16 + m))
                    for (b, c, n) in linear(wo, kcn, mo, lambda kc, c, n, mb=mobase: mo.t[:, mb + kc, c:c + n]):
                        if half == 0:
                            tt(tms, tms.t[:, c:c + n], b.t[:, 0:n], sg.t[:, c:c + n], ALU.mult, [b, sg])
                        else:
                            tm2 = tmr.next()
                            tt(tm2, tm2.t[:, c:c + n], b.t[:, 0:n], sg.t[:, c:c + n], ALU.mult, [b, sg])
                            tt(sqm, sqm.t[:, m, c:c + n], tms.t[:, c:c + n], tm2.t[:, c:c + n], ALU.add, [tms, tm2])
            lagq = []
            for m in range(16):
                w = wload(w_out_d[m], 16)
                for (b, c, n) in linear(w, 16, sqm, lambda kc, c, n: sqm.t[:, kc, c:c + n]):
                    tt(xs, xs.t[:, m, c:c + n], b.t[:, 0:n], xs.t[:, m, c:c + n], ALU.add, [b, xs])
                sumsq_then(m, lagq, segs)
                act(hsc[m], hs.t[:, m, lo:NC], xs.t[:, m, lo:NC], AF.Copy, [xs, vecs], scale=vcol(V_FFN_G + m))
            while lagq:
                lagq.pop(0)()
            rs = rsr.next()
            rstd_cols(rs, segs)
            for kc in range(16):
                tt(hsc[kc], hs.t[:, kc, lo:NC], hs.t[:, kc, lo:NC], rs.t[:, lo:NC], ALU.mult, [hsc[kc], rs])
            lagq = []
            def ffn_chunk(ci, dst_T, dst_slot):
                accs = []
                for which, mo_i in ((0, ci), (1, 43 + ci)):
                    w = wload(w_up_d[mo_i], 16)
                    u0 = u0r.next()
                    if it > 0:
                        act(u0, u0.t[:, 0:2], carry.t[:, mo_i, :], AF.Copy, [carry])
                    for (b, c, n) in linear(w, 16, hsc, lambda kc, c, n: hs.t[:, kc, c:c + n]):
                        act(u0, u0.t[:, c:c + n], b.t[:, 0:n], AF.Copy, [b])
                    if it < 3:
                        act(carry, carry.t[:, mo_i, :], u0.t[:, 512:514], AF.Copy, [u0])
                    ac = acr.next()
                    cw = V_CONVW
                    act(ac, ac.t[:], u0.t[:, 2:514], AF.Identity, [u0, vecs], bias=vcol(V_CONVB + mo_i),
                        scale=vcol(cw + 2 * 86 + mo_i))
                    stt(ac, ac.t[:], u0.t[:, 1:513], vcol(cw + 1 * 86 + mo_i), ac.t[:], ALU.mult, ALU.add, [u0, ac, vecs])
                    stt(ac, ac.t[:], u0.t[:, 0:512], vcol(cw + 0 * 86 + mo_i), ac.t[:], ALU.mult, ALU.add, [u0, ac, vecs])
                    accs.append(ac)
                sl_ = tmr.next()
                act(sl_, sl_.t[:, 0:512], accs[1].t[:], AF.Silu, [accs[1]])
                tt(dst_T, dst_T.t[:, dst_slot, :], sl_.t[:, 0:512], accs[0].t[:], ALU.mult, [sl_, accs[0]])

            def w_down(hf, kcs, wdn, src_of_kc, last):
                for m in range(16):
                    wd = wdr.next()
                    P.dma("pool", wd.t[:, 0:len(kcs), :],
                          wdn[m].rearrange("p (k m) -> p k m", m=128)[:, kcs[0]:kcs[-1] + 1, :], writes=[wd],
                          key="wd4_%d" % (wc[1] % 3))
                    wc[1] += 1
                    b = pb.next()
                    for i, kc in enumerate(kcs):
                        src_T, src_ap = src_of_kc(kc)
                        mm(b, b.t[:, :], wd.t[:, i, :], src_ap, i == 0, i == len(kcs) - 1, [wd, src_T])
                    tt(xs, xs.t[:, m, 2:NC], b.t[:, :], xs.t[:, m, 2:NC], ALU.add, [b, xs])
                    if last:
                        sumsq_then(m, lagq, [(2, 512)])

            for ci in range(0, 22):
                ffn_chunk(ci, hid, ci)
            for ci in range(22, 24):
                ffn_chunk(ci, hidb, ci - 22)
            w_down(0, list(range(22)), w_dn0_d, lambda kc: (hid, hid.t[:, kc, :]), False)
            for ci in range(24, 43):
                ffn_chunk(ci, hid, ci - 24)
            if it < 3:
                load_h_mo(it + 1)
            src1 = lambda kc: (hidb, hidb.t[:, kc, :]) if kc < 2 else (hid, hid.t[:, kc - 2, :])
            w_down(1, list(range(0, 19)), w_dn1_d, src1, False)
            w_down(1, list(range(19, 21)), w_dn1_d, src1, True)
            while lagq:
                lagq.pop(0)()
            rs2 = rsr.next()
            rstd_cols(rs2, [(2, 512)])
            for kc in range(16):
                stt(xs, xs.t[:, kc, 2:NC], xs.t[:, kc, 2:NC], vcol(V_FIN_G + kc), rs2.t[:, 2:NC], ALU.mult, ALU.mult,
                    [xs, rs2, vecs])
            P.dma("sp", outT[it], xs.t[:, :, 2:NC], reads=[xs], key="out4")
            if it < 3:
                load_x(it + 1)
        P.end_phase()
    return nc, es, P


def _wtile(W):
    K, M = W.shape
    return np.ascontiguousarray(W.reshape(K // 128, 128, M // 128, 128).transpose(2, 1, 0, 3)).reshape(M // 128, 128, (K // 128) * 128)


def _pvec(v):
    return np.asarray(v, np.float32).reshape(-1, 128).T


def _constants():
    tri = (np.arange(128)[:, None] <= np.arange(128)[None, :]).astype(np.float32)
    ident = np.eye(128, dtype=np.float32)
    slopes = (2.0 ** (-8.0 * np.arange(1, 13, dtype=np.float32) / 12.0)).astype(np.float32)
    k = np.arange(128, dtype=np.float32)[:, None]
    q = np.arange(128, dtype=np.float32)[None, :]
    bm = np.zeros((128, 12, 2, 128), np.float32)
    for g in range(3):
        for s in range(4):
            hd = g * 4 + s
            sl = slopes[hd] * DILS[g]
            bm[:, hd, 0, :] = np.where(k <= q, -sl * (q - k), NEG)
            bm[:, hd, 1, :] = np.where(k >= q, -sl * (q + 128.0 - k), NEG)
    return tri, ident, bm.reshape(128, -1)


def _rope_tables(half):
    pos = np.arange(SEQ, dtype=np.float64) - (0.0 if half == 1 else float(NOWN))
    pos = np.maximum(pos, 0.0)
    inv_freq = 10000.0 ** (-np.arange(0, 64, 2, dtype=np.float64) / 64.0)
    ang = pos[None, :] * inv_freq[:, None]
    c, s = np.cos(ang).astype(np.float32), np.sin(ang).astype(np.float32)
    return np.concatenate([c, c], 0), np.concatenate([-s, s], 0)


_CACHE = {}


def kernel(x, attn_norm_g, w_in, b_gate, q_norm_g, w_uq, kv_norm_g, w_ukv, w_o_mla, w_o_dil, w_out, ffn_norm_g,
           w_up, conv_w, conv_b, w_down, final_norm_g):
    f = lambda a: np.asarray(a, np.float32)
    x, w_in, w_uq, w_ukv = f(x), f(w_in)[0], f(w_uq)[0], f(w_ukv)[0]
    swap = np.concatenate([np.arange(32, 64), np.arange(0, 32)])
    w_lat = _wtile(np.concatenate([w_in[:, 0:832], w_in[:, 768:832][:, swap]], 1))
    w_dil = _wtile(w_in[:, 832:5440])
    w_gate = _wtile(w_in[:, 5440:9536])
    uq = w_uq.reshape(512, 8, 192)
    uq_ext = np.concatenate([uq[:, :, 0:128], uq[:, :, 128:192], uq[:, :, 128:192][:, :, swap]], 2).reshape(512, 8 * 256)
    shared = {
        "w_lat": w_lat, "w_dil": w_dil, "w_gate": w_gate, "w_uq": _wtile(uq_ext), "w_ukv": _wtile(w_ukv),
        "w_omla": _wtile(f(w_o_mla)[0]), "w_odil": _wtile(f(w_o_dil)[0]), "w_out": _wtile(f(w_out)[0]),
        "w_up": _wtile(f(w_up)[0]), "w_dn0": _wtile(f(w_down)[0][0:2816]), "w_dn1": _wtile(f(w_down)[0][2816:]),
    }
    tri, ident, bm = _constants()
    shared.update({"tri": tri, "ident": ident, "biasmat": bm})
    vec_common = np.concatenate([
        _pvec(f(attn_norm_g)[0]), _pvec(f(ffn_norm_g)[0]), _pvec(f(final_norm_g)), _pvec(f(q_norm_g)[0]),
        _pvec(f(kv_norm_g)[0]), _pvec(f(b_gate)[0]), _pvec(f(conv_b)[0]),
        f(conv_w)[0].reshape(3, 86, 128).transpose(2, 0, 1).reshape(128, 258)], 1)
    in_maps = []
    for c in range(8):
        b, half = c // 2, c % 2
        if half == 1:
            xl = x[b]
        else:
            xl = np.concatenate([np.zeros((NOWN, D), np.float32), x[b, :NOWN]], 0)
        xt = np.ascontiguousarray(xl.reshape(16, 256, 16, 128).transpose(0, 3, 2, 1))
        extra = np.zeros((128, 3), np.float32)
        extra[:, 0] = 0.0 if half == 1 else NEG
        extra[:, 1] = EPS
        extra[:, 2] = 1e-30
        cosT, sinT = _rope_tables(half)
        m = dict(shared)
        m.update({"xT": xt, "vecs": np.ascontiguousarray(np.concatenate([vec_common, extra], 1)), "cosT": cosT, "sinT": sinT})
        in_maps.append(m)
    if _CACHE.get("maps_only"):
        return in_maps
    if "nc" not in _CACHE:
        _CACHE["nc"] = build_program()
    nc = _CACHE["nc"][0]
    res = run_bass_kernel_spmd(nc, in_maps, core_ids=list(range(8)))
    out = np.empty((4, SEQ, D), np.float32)
    for c in range(8):
        b, half = c // 2, c % 2
        o = res.results[c]["outT"]
        out[b, half * NOWN:(half + 1) * NOWN] = o.transpose(0, 3, 2, 1).reshape(NOWN, D)
    return out
```

```python
import numpy as np
from contextlib import ExitStack
import concourse.bass as bass
import concourse.mybir as mybir
from concourse.bass_utils import run_bass_kernel_spmd

F32, BF16 = mybir.dt.float32, mybir.dt.bfloat16
AF = mybir.ActivationFunctionType
ALU = mybir.AluOpType

D = 2048
SEQ = 4096
NOWN = 2048
HALO0 = 2046
NEXT = NOWN + 2
D_FF = 5504
EPS = 1e-6
NEG = -30000.0
SC_MLA = 192.0 ** -0.5
SC_DIL = 128.0 ** -0.5
DILS = (1, 4, 16)

V_ATTN_G, V_FFN_G, V_FIN_G, V_Q_G, V_KV_G, V_BGATE, V_CONVB, V_CONVW, V_CTX, V_EPS, V_TINY, NVEC = \
    0, 16, 32, 48, 52, 54, 86, 172, 430, 431, 432, 433


DBG = {}


class T:
    __slots__ = ("name", "t", "lw", "rd")

    def __init__(self, name, t):
        self.name, self.t, self.lw, self.rd = name, t, None, []


class View:
    def __init__(self, parent, ap):
        self.p, self.t = parent, ap
    lw = property(lambda s: s.p.lw, lambda s, v: setattr(s.p, "lw", v))
    rd = property(lambda s: s.p.rd, lambda s, v: setattr(s.p, "rd", v))


class Op:
    __slots__ = ("eng", "fn", "reads", "writes", "dma", "key", "deps", "need_inc", "sem", "val",
                 "idx", "phase", "xdeps", "par")

    def __init__(self, eng, fn, reads, writes, dma, key, xdeps=None):
        self.eng, self.fn, self.reads, self.writes, self.dma, self.key = eng, fn, reads, writes, dma, key
        self.deps, self.need_inc, self.sem, self.val, self.xdeps = [], False, None, 0, xdeps or []
        self.par = False


class Prog:
    ENGS = ("pe", "act", "dve", "pool", "sp")

    def __init__(self, nc, es):
        self.nc, self.es = nc, es
        self.eobj = {"pe": nc.tensor, "act": nc.scalar, "dve": nc.vector, "pool": nc.gpsimd, "sp": nc.sync}
        self.esem = {e: es.enter_context(nc.semaphore("sem_" + e)) for e in self.ENGS}
        self.ecnt = {e: 0 for e in self.ENGS}
        self.ksem, self.kcnt = {}, {}
        self.waited = {e: {} for e in self.ENGS}
        self.ops, self.nops, self.phase = [], 0, 0
        self.last = {e: None for e in self.ENGS}
        self.pending_dma = []
        self.stores = []
        self.stats = {e: 0 for e in self.ENGS}

    def op(self, eng, fn, reads=(), writes=(), dma=False, key=None, xdeps=None):
        o = Op(eng, fn, list(reads), list(writes), dma, key, xdeps)
        o.idx, o.phase = self.nops, self.phase
        self.nops += 1
        self.ops.append(o)
        return o

    def dma(self, eng, out_ap, in_ap, reads=(), writes=(), key=None, par=False, **kw):
        assert key is not None
        o = self.op(eng, lambda E: E.dma_start(out=out_ap, in_=in_ap, **kw), reads, writes, True, key)
        o.par = par
        return o

    def _sem_for_key(self, key):
        if key not in self.ksem:
            self.ksem[key] = self.es.enter_context(self.nc.semaphore("k_" + key))
            self.kcnt[key] = 0
        return self.ksem[key]

    def end_phase(self):
        ops = self.ops
        for o in ops:
            deps = set(o.xdeps)
            for t in o.reads:
                if t.lw is not None:
                    deps.add(t.lw)
                t.rd.append(o)
            for t in o.writes:
                if t.lw is not None:
                    deps.add(t.lw)
                for r in t.rd:
                    deps.add(r)
                t.rd = []
                t.lw = o
            deps.discard(o)
            keep, best = [], {}
            for d in deps:
                if d.phase < o.phase:
                    continue
                if d.dma:
                    if o.dma and o.par and d.key == o.key and d.eng == o.eng:
                        continue
                    keep.append(d)
                elif d.eng == "pe" and o.eng == "pe" and not o.dma:
                    continue
                else:
                    b = best.get(d.eng)
                    if b is None or d.idx > b.idx:
                        best[d.eng] = d
            keep.extend(best.values())
            o.deps = keep
            for d in keep:
                d.need_inc = True
            if o.dma:
                self.pending_dma.append(o)
            self.last[o.eng] = o
        bar_deps = [x for x in self.last.values() if x is not None and x.phase == self.phase and not x.dma]
        for d in bar_deps:
            d.need_inc = True
        bar_deps = bar_deps + [d for d in self.pending_dma if d.phase == self.phase]
        bars = []
        for e in self.ENGS:
            b = Op(e, None, [], [], False, None)
            b.idx, b.phase, b.deps = self.nops, self.phase, bar_deps
            self.nops += 1
            bars.append(b)
        for o in ops + bars:
            E = self.eobj[o.eng]
            w = self.waited[o.eng]
            need = {}
            for d in o.deps:
                sid = id(d.sem)
                if sid not in need or need[sid][1] < d.val:
                    need[sid] = (d.sem, d.val)
            for sid, (sem, val) in need.items():
                if w.get(sid, 0) < val:
                    E.wait_ge(sem, val)
                    w[sid] = val
            if o.fn is None:
                continue
            if o.dma:
                o.sem = self._sem_for_key(o.key)
                self.kcnt[o.key] += 16
                o.val = self.kcnt[o.key]
                ins = o.fn(E)
                ins.then_inc(o.sem, 16)
            else:
                ins = o.fn(E)
                if o.need_inc:
                    self.ecnt[o.eng] += 1
                    o.sem, o.val = self.esem[o.eng], self.ecnt[o.eng]
                    ins.then_inc(o.sem, 1)
            self.stats[o.eng] += 1
        self.ops = []
        self.pending_dma = []
        self.phase += 1


class Ring:
    def __init__(self, tiles):
        self.tiles, self.i = tiles, 0

    def next(self):
        t = self.tiles[self.i % len(self.tiles)]
        self.i += 1
        return t


def build_program(stop_after=99, dbg=False):
    nc = bass.Bass("TRN2", target_bir_lowering=False)
    es = ExitStack()
    P = Prog(nc, es)

    def dram(name, shape, dt, kind):
        return nc.dram_tensor(name, list(shape), dt, kind=kind).ap()

    xT = dram("xT", [16, 128, 16, 256], F32, "ExternalInput")
    vecs_d = dram("vecs", [128, NVEC], F32, "ExternalInput")
    cos_d = dram("cosT", [64, SEQ], F32, "ExternalInput")
    sin_d = dram("sinT", [64, SEQ], F32, "ExternalInput")
    tri_d = dram("tri", [128, 128], F32, "ExternalInput")
    bm_d = dram("biasmat", [128, 12 * 2 * 128], F32, "ExternalInput")
    ident_d = dram("ident", [128, 128], F32, "ExternalInput")
    w_lat_d = dram("w_lat", [7, 128, 16 * 128], F32, "ExternalInput")
    w_dil_d = dram("w_dil", [36, 128, 16 * 128], F32, "ExternalInput")
    w_gate_d = dram("w_gate", [32, 128, 16 * 128], F32, "ExternalInput")
    w_uq_d = dram("w_uq", [16, 128, 4 * 128], F32, "ExternalInput")
    w_ukv_d = dram("w_ukv", [16, 128, 2 * 128], F32, "ExternalInput")
    w_omla_d = dram("w_omla", [16, 128, 8 * 128], F32, "ExternalInput")
    w_odil_d = dram("w_odil", [16, 128, 4 * 128], F32, "ExternalInput")
    w_out_d = dram("w_out", [16, 128, 16 * 128], F32, "ExternalInput")
    w_up_d = dram("w_up", [86, 128, 16 * 128], F32, "ExternalInput")
    w_dn0_d = dram("w_dn0", [16, 128, 22 * 128], F32, "ExternalInput")
    w_dn1_d = dram("w_dn1", [16, 128, 21 * 128], F32, "ExternalInput")
    outT = dram("outT", [4, 128, 16, 512], F32, "ExternalOutput")
    Hs = dram("Hs", [16, 128, 16, 256], BF16, "Internal")
    MOs = dram("MOs", [12, 128, NEXT], BF16, "Internal")
    Hs_T = T("Hs", None)
    MOs_T = T("MOs", None)

    def sb(ctx, name, shape, dt):
        return T(name, ctx.enter_context(nc.sbuf_tensor(name, list(shape), dt)))

    banks = [T("bank%d" % i, es.enter_context(nc.psum_tensor("bank%d" % i, [128, 512], F32))) for i in range(7)]
    tbank_t = es.enter_context(nc.psum_tensor("tbank", [128, 1024], BF16))
    tbank = T("tbank", tbank_t)
    tps = Ring([View(tbank, tbank_t[:, i * 128:(i + 1) * 128]) for i in range(8)])

    vecs = sb(es, "vecs_sb", [128, NVEC], F32)
    ones = sb(es, "ones_sb", [128, 128], BF16)
    tri = sb(es, "tri_sb", [128, 128], BF16)
    ident = sb(es, "ident_sb", [128, 128], BF16)
    P.dma("sp", vecs.t[:], vecs_d, writes=[vecs], key="c_vecs")
    P.dma("pool", tri.t[:], tri_d, writes=[tri], key="c_tri")
    P.dma("pool", ident.t[:], ident_d, writes=[ident], key="c_ident")
    P.op("dve", lambda E: E.memset(ones.t[:], 1.0), writes=[ones])
    ones32 = sb(es, "ones32_sb", [128, 128], F32)
    P.op("dve", lambda E: E.memset(ones32.t[:], 1.0), writes=[ones32])

    def dump(name, tile, shape2):
        d = dram("dbg_" + name, shape2, F32, "ExternalOutput")
        src = tile.t[:]
        if len(src.shape) == 3:
            d = d.rearrange("p (a b) -> p a b", a=src.shape[1])
        elif len(src.shape) == 4:
            d = d.rearrange("p (a b c) -> p a b c", a=src.shape[1], b=src.shape[2])
        P.dma("pool", d, src, reads=[tile], key="dbg_" + name)

    def sl(start, count, step):
        return slice(start, start + (count - 1) * step + 1, step)

    def vcol(c, n=128):
        return vecs.t[0:n, c:c + 1]

    def mm(out_T, out_ap, lhsT_ap, rhs_ap, start, stop, reads):
        P.op("pe", lambda E: E.matmul(out_ap, lhsT=lhsT_ap, rhs=rhs_ap, start=start, stop=stop),
             reads=reads, writes=[out_T])

    def act(out_T, out_ap, in_ap, func, reads, bias=None, scale=None):
        kw = {}
        if bias is not None:
            kw["bias"] = bias
        if scale is not None:
            kw["scale"] = scale
        P.op("act", lambda E: E.activation(out=out_ap, in_=in_ap, func=func, **kw), reads=reads, writes=[out_T])

    def tt(out_T, out_ap, in0, in1, op, reads, eng="dve"):
        P.op(eng, lambda E: E.tensor_tensor(out=out_ap, in0=in0, in1=in1, op=op), reads=reads, writes=[out_T])

    def stt(out_T, out_ap, in0, scalar, in1, op0, op1, reads):
        P.op("dve", lambda E: E.scalar_tensor_tensor(out=out_ap, in0=in0, scalar=scalar, in1=in1, op0=op0, op1=op1),
             reads=reads, writes=[out_T])

    def ts(out_T, out_ap, in0, s1, s2, op0, op1, reads):
        if s2 is None:
            P.op("dve", lambda E: E.tensor_scalar(out=out_ap, in0=in0, scalar1=s1, scalar2=None, op0=op0),
                 reads=reads, writes=[out_T])
        else:
            P.op("dve", lambda E: E.tensor_scalar(out=out_ap, in0=in0, scalar1=s1, scalar2=s2, op0=op0, op1=op1),
                 reads=reads, writes=[out_T])

    def recip(out_T, ap, reads):
        P.op("dve", lambda E: E.reciprocal(out=ap, in_=ap), reads=reads, writes=[out_T])

    def rstd_from_ss(ss_T, n, rs_T, inv_dim):
        act(rs_T, rs_T.t[:, 0:n], ss_T.t[:, 0:n], AF.Ln, [ss_T, vecs], bias=vcol(V_EPS), scale=inv_dim)
        act(rs_T, rs_T.t[:, 0:n], rs_T.t[:, 0:n], AF.Exp, [rs_T], scale=-0.5)

    TW = 256
    lat = ExitStack()
    cqn = sb(lat, "cqn", [128, 4, NEXT], BF16)
    ckvn = sb(lat, "ckvn", [128, 2, SEQ], BF16)
    kpe = sb(lat, "kpe", [64, SEQ], BF16)
    with ExitStack() as ph:
        wl = [sb(ph, "wl%d" % i, [128, 16, 128], BF16) for i in range(7)]
        for i in range(7):
            P.dma("pool", wl[i].t[:], w_lat_d[i].rearrange("p (k m) -> p k m", m=128), writes=[wl[i]], key="wl%d" % i)
        xr = Ring([sb(ph, "x1_%d" % i, [128, 16, TW], F32) for i in range(3)])
        sqr = Ring([sb(ph, "sq1_%d" % i, [128, 16, TW], BF16) for i in range(2)])
        hr = Ring([sb(ph, "h1_%d" % i, [128, 16, TW], BF16) for i in range(4)])
        rsr = Ring([sb(ph, "rs1_%d" % i, [128, TW], F32) for i in range(5)])
        raw = Ring([sb(ph, "raw1_%d" % i, [128, 4, TW], F32) for i in range(2)])
        sqs = Ring([sb(ph, "sqs1_%d" % i, [128, 4, TW], BF16) for i in range(2)])
        csr = Ring([sb(ph, "cs1_%d" % i, [64, 2, TW], F32) for i in range(4)])
        tmr = Ring([sb(ph, "tm1_%d" % i, [64, 2, TW], F32) for i in range(2)])
        pb = Ring(banks[0:5])
        ssb = Ring(banks[5:7])
        def stage_n(j):
            t0 = j * TW
            jt, jo = t0 // 512, t0 % 512
            xt = xr.next()
            P.dma("sp", xt.t[:], xT[j], writes=[xt], key="x1_%d" % (j % 3))
            cs = csr.next()
            P.dma("sp", cs.t[:, 0, :], cos_d[:, t0:t0 + TW], writes=[cs], key="cs1a_%d" % (j % 4))
            P.dma("sp", cs.t[:, 1, :], sin_d[:, t0:t0 + TW], writes=[cs], key="cs1b_%d" % (j % 4))
            sq = sqr.next()
            act(sq, sq.t[:], xt.t[:], AF.Square, [xt])
            ss = ssb.next()
            for kc in range(16):
                mm(ss, ss.t[:, 0:TW], ones.t[:], sq.t[:, kc, :], kc == 0, kc == 15, [ones, sq])
            rs = rsr.next()
            rstd_from_ss(ss, TW, rs, 1.0 / D)
            ht = hr.next()
            for kc in range(16):
                stt(ht, ht.t[:, kc, :], xt.t[:, kc, :], vcol(V_ATTN_G + kc), rs.t[:, 0:TW], ALU.mult, ALU.mult, [xt, rs, vecs])
            P.dma("pool", Hs[j], ht.t[:], reads=[ht], key="h1_%d" % (j % 4))
            return ht, cs

        def stage_l(j, ht, cs):
            t0 = j * TW

            def latent(chunks, gcol, nfeat, dst, dst_col, c0, n):
                rw, sqq = raw.next(), sqs.next()
                for ci, ch in enumerate(chunks):
                    b = pb.next()
                    for kc in range(16):
                        mm(b, b.t[:, 0:n], wl[ch].t[:, kc, :], ht.t[:, kc, c0:c0 + n], kc == 0, kc == 15, [wl[ch], ht])
                    act(rw, rw.t[:, ci, 0:n], b.t[:, 0:n], AF.Copy, [b])
                    act(sqq, sqq.t[:, ci, 0:n], b.t[:, 0:n], AF.Square, [b])
                s2 = ssb.next()
                for ci in range(len(chunks)):
                    mm(s2, s2.t[:, 0:n], ones.t[:], sqq.t[:, ci, 0:n], ci == 0, ci == len(chunks) - 1, [ones, sqq])
                r2 = rsr.next()
                rstd_from_ss(s2, n, r2, 1.0 / nfeat)
                for ci in range(len(chunks)):
                    stt(dst, dst.t[:, ci, dst_col:dst_col + n], rw.t[:, ci, 0:n], vcol(gcol + ci), r2.t[:, 0:n],
                        ALU.mult, ALU.mult, [rw, r2, vecs])

            if t0 + TW > HALO0:
                c0 = max(HALO0 - t0, 0)
                latent([0, 1, 2, 3], V_Q_G, 512, cqn, t0 + c0 - HALO0, c0, TW - c0)
            latent([4, 5], V_KV_G, 256, ckvn, t0, 0, TW)
            ba, bb = pb.next(), pb.next()
            for kc in range(16):
                mm(ba, ba.t[0:64, 0:TW], wl[6].t[:, kc, 0:64], ht.t[:, kc, :], kc == 0, kc == 15, [wl[6], ht])
            for kc in range(16):
                mm(bb, bb.t[0:64, 0:TW], wl[6].t[:, kc, 64:128], ht.t[:, kc, :], kc == 0, kc == 15, [wl[6], ht])
            tm = tmr.next()
            tt(tm, tm.t[:, 0, :], ba.t[0:64, 0:TW], cs.t[:, 0, :], ALU.mult, [ba, cs])
            tt(tm, tm.t[:, 1, :], bb.t[0:64, 0:TW], cs.t[:, 1, :], ALU.mult, [bb, cs])
            tt(kpe, kpe.t[:, t0:t0 + TW], tm.t[:, 0, :], tm.t[:, 1, :], ALU.add, [tm])
        NT1 = SEQ // TW
        LOOK1 = 2
        staged = [stage_n(j) for j in range(LOOK1)]
        for j in range(NT1):
            if j + LOOK1 < NT1:
                staged.append(stage_n(j + LOOK1))
            stage_l(j, *staged.pop(0))
        if dbg and stop_after == 1:
            dump("cqn", cqn, [128, 4 * NEXT])
            dump("ckvn", ckvn, [128, 2 * SEQ])
            dump("kpe", kpe, [64, SEQ])
        P.end_phase()
    if stop_after == 1:
        return nc, es, P

    QG = [(HALO0, 2)] + [(NOWN + 512 * i, 512) for i in range(4)]
    with ExitStack() as ph:
        wuq_all = sb(ph, "wuq", [128, 16, 4, 128], BF16)
        wukv_all = sb(ph, "wukv", [128, 16, 2, 128], BF16)
        P.dma("pool", wukv_all.t[:], w_ukv_d.rearrange("o p (k m) -> p o k m", m=128), writes=[wukv_all], key="wukv")
        P.dma("pool", wuq_all.t[:], w_uq_d.rearrange("o p (k m) -> p o k m", m=128), writes=[wuq_all], key="wuq")

        wuq = [View(wuq_all, wuq_all.t[:, i]) for i in range(16)]
        wukv = [View(wukv_all, wukv_all.t[:, i]) for i in range(16)]
        cosq = sb(ph, "cosq", [64, NEXT], F32)
        sinq = sb(ph, "sinq", [64, NEXT], F32)
        P.dma("sp", cosq.t[:], cos_d[:, HALO0:SEQ], writes=[cosq], key="cosq")
        P.dma("sp", sinq.t[:], sin_d[:, HALO0:SEQ], writes=[sinq], key="sinq")
        khr = Ring([sb(ph, "kh%d" % i, [128, SEQ], BF16) for i in range(2)])
        vhr = Ring([sb(ph, "vh%d" % i, [128, SEQ], BF16) for i in range(2)])
        qnr = Ring([sb(ph, "qn%d" % i, [128, NEXT], BF16) for i in range(2)])
        qpr = Ring([sb(ph, "qp%d" % i, [64, NEXT], BF16) for i in range(2)])
        ptr = Ring([sb(ph, "pt%d" % i, [128, 512], BF16) for i in range(8)])
        tmr = Ring([sb(ph, "tm2_%d" % i, [64, 2, 512], F32) for i in range(2)])
        rzr = Ring([sb(ph, "rz%d" % i, [128, 512], F32) for i in range(2)])
        osr = Ring([sb(ph, "os%d" % i, [128, 512], BF16) for i in range(2)])
        pb = Ring(banks[0:3])
        ob = Ring([(banks[3], banks[4]), (banks[5], banks[6])])
        pend, LOOK = [], 6
        for h in range(8):
            wk, wv, wqn, wqp = wukv[2 * h], wukv[2 * h + 1], wuq[2 * h], wuq[2 * h + 1]
            Kh, Vh, qn, qp = khr.next(), vhr.next(), qnr.next(), qpr.next()
            for tg in range(8):
                b = pb.next()
                for kc in range(2):
                    mm(b, b.t[:, :], wk.t[:, kc, :], ckvn.t[:, kc, tg * 512:(tg + 1) * 512], kc == 0, kc == 1, [wk, ckvn])
                act(Kh, Kh.t[:, tg * 512:(tg + 1) * 512], b.t[:, :], AF.Copy, [b])
            for tg in range(8):
                b = pb.next()
                for i in range(4):
                    tb = tg * 4 + i
                    for kc in range(2):
                        mm(b, b.t[:, i * 128:(i + 1) * 128], ckvn.t[:, kc, tb * 128:(tb + 1) * 128], wv.t[:, kc, :],
                           kc == 0, kc == 1, [wv, ckvn])
                act(Vh, Vh.t[:, tg * 512:(tg + 1) * 512], b.t[:, :], AF.Copy, [b])
            for (q0, n) in QG:
                c0 = q0 - HALO0
                b = pb.next()
                for kc in range(4):
                    mm(b, b.t[:, 0:n], wqn.t[:, kc, :], cqn.t[:, kc, c0:c0 + n], kc == 0, kc == 3, [wqn, cqn])
                act(qn, qn.t[:, c0:c0 + n], b.t[:, 0:n], AF.Copy, [b])
                ba, bb = pb.next(), pb.next()
                for kc in range(4):
                    mm(ba, ba.t[0:64, 0:n], wqp.t[:, kc, 0:64], cqn.t[:, kc, c0:c0 + n], kc == 0, kc == 3, [wqp, cqn])
                for kc in range(4):
                    mm(bb, bb.t[0:64, 0:n], wqp.t[:, kc, 64:128], cqn.t[:, kc, c0:c0 + n], kc == 0, kc == 3, [wqp, cqn])
                tm = tmr.next()
                tt(tm, tm.t[:, 0, 0:n], ba.t[0:64, 0:n], cosq.t[:, c0:c0 + n], ALU.mult, [ba, cosq])
                tt(tm, tm.t[:, 1, 0:n], bb.t[0:64, 0:n], sinq.t[:, c0:c0 + n], ALU.mult, [bb, sinq])
                tt(qp, qp.t[:, c0:c0 + n], tm.t[:, 0, 0:n], tm.t[:, 1, 0:n], ALU.add, [tm])
            for (q0, n) in QG:
                kb_last = (q0 + n - 1) // 128
                Ob, Zb = ob.next()
                for kb in range(kb_last + 1):
                    k0 = kb * 128
                    qlo = max(q0, k0)
                    nn = q0 + n - qlo
                    cl = qlo - HALO0
                    S = pb.next()
                    mm(S, S.t[:, 0:nn], Kh.t[:, k0:k0 + 128], qn.t[:, cl:cl + nn], True, False, [Kh, qn])
                    mm(S, S.t[:, 0:nn], kpe.t[0:64, k0:k0 + 128], qp.t[0:64, cl:cl + nn], False, True, [kpe, qp])
                    Pt = ptr.next()
                    act(Pt, Pt.t[:, 0:nn], S.t[:, 0:nn], AF.Exp, [S, vecs],
                        bias=(vcol(V_CTX) if k0 < NOWN else None), scale=SC_MLA)
                    if qlo < k0 + 128:
                        m = min(q0 + n, k0 + 128) - qlo
                        off = qlo - k0
                        tt(Pt, Pt.t[:, 0:m], Pt.t[:, 0:m], tri.t[:, off:off + m], ALU.mult, [Pt, tri])

                    def stage_b(Ob=Ob, Zb=Zb, Vh=Vh, Pt=Pt, k0=k0, qlo=qlo, q0=q0, n=n, nn=nn, kb=kb, kb_last=kb_last, h=h):
                        mm(Ob, Ob.t[:, qlo - q0:n], Vh.t[:, k0:k0 + 128], Pt.t[:, 0:nn], kb == 0, kb == kb_last, [Vh, Pt])
                        mm(Zb, Zb.t[:, qlo - q0:n], ones.t[:], Pt.t[:, 0:nn], kb == 0, kb == kb_last, [ones, Pt])
                        if kb == kb_last:
                            rz, osb = rzr.next(), osr.next()
                            act(rz, rz.t[:, 0:n], Zb.t[:, 0:n], AF.Ln, [Zb, vecs], bias=vcol(V_TINY))
                            act(rz, rz.t[:, 0:n], rz.t[:, 0:n], AF.Exp, [rz], scale=-1.0)
                            tt(osb, osb.t[:, 0:n], Ob.t[:, 0:n], rz.t[:, 0:n], ALU.mult, [Ob, rz])
                            P.dma("sp", MOs[h][:, q0 - HALO0:q0 - HALO0 + n], osb.t[:, 0:n], reads=[osb],
                                  key="os%d" % ((osr.i - 1) % 2))
                    pend.append(stage_b)
                    while len(pend) > LOOK:
                        pend.pop(0)()
        while pend:
            pend.pop(0)()
        P.end_phase()
    if stop_after == 2:
        with ExitStack() as ph:
            mo_dbg = sb(ph, "mo_dbg", [128, 12, NEXT], BF16)
            P.dma("sp", mo_dbg.t[:], MOs.rearrange("c p t -> p c t"), writes=[mo_dbg], key="modbg")
            dump("mo", mo_dbg, [128, 12 * NEXT])
            P.end_phase()
        return nc, es, P

    lat.close()
    with ExitStack() as ph:
        bm = sb(ph, "bm", [128, 12, 256], F32)
        hr = Ring([sb(ph, "h3_%d" % i, [128, 2, 16, 256], BF16) for i in range(2)])
        wr = Ring([sb(ph, "w3_%d" % i, [128, 16, 128], BF16) for i in range(12)])
        qT = [sb(ph, "dq%d" % g, [128, 2560], BF16) for g in range(3)]
        KBASE = (1536, 1024, 0)
        kT = [sb(ph, "dk%d" % g, [128, SEQ - KBASE[g]], BF16) for g in range(3)]
        vT = [sb(ph, "dv%d" % g, [128, SEQ - KBASE[g]], BF16) for g in range(3)]
        Uacc = sb(ph, "uacc", [128, NEXT], F32)
        Zacc = sb(ph, "zacc", [128, NEXT], F32)
        dout = sb(ph, "dout", [128, NEXT], BF16)
        vbr = Ring([sb(ph, "vb%d" % i, [128, 128], BF16) for i in range(24)])
        ptr = Ring([sb(ph, "pt3_%d" % i, [128, 256], BF16) for i in range(10)])
        tmr = Ring([sb(ph, "tm3_%d" % i, [128, 256], F32) for i in range(4)])
        pb = Ring(banks[0:4])
        sring = Ring(banks[0:3])
        ob = Ring([(banks[3], banks[4]), (banks[5], banks[6])])
        pend, LOOK = [], 6
        wcnt = [0]
        def prefetch(s):
            wt = {}
            for kind in range(3):
                for g in range(3):
                    w = wr.next()
                    mo = kind * 12 + g * 4 + s
                    P.dma("pool", w.t[:], w_dil_d[mo].rearrange("p (k m) -> p k m", m=128), writes=[w],
                          key="w3_%d" % (wcnt[0] % 12))
                    wcnt[0] += 1
                    wt[(kind, g)] = w
            hts = [load_h(0), load_h(1)]
            return wt, hts

        def load_h(j):
            ht = hr.next()
            P.dma("sp", ht.t[:], Hs[2 * j:2 * j + 2].rearrange("a p k t -> p a k t"), writes=[ht], key="h3_%d" % ((hr.i - 1) % 2))
            return ht

        pf = prefetch(0)
        P.dma("sp", bm.t[:], bm_d.rearrange("p (h c) -> p h c", h=12), writes=[bm], key="bm")
        for s in range(4):
            wt, hts = pf
            for j in range(8):
                t0 = j * 512
                need = [(kind, g) for g in range(3) for kind in (1, 2) if t0 >= KBASE[g]]
                if t0 >= 1536:
                    need += [(0, g) for g in range(3)]
                ht = hts[j] if j < 2 else load_h(j)
                for (kind, g) in need:
                    w = wt[(kind, g)]
                    b = pb.next()
                    for kc in range(16):
                        mm(b, b.t[:, :], w.t[:, kc, :], ht.t[:, :, kc, :], kc == 0, kc == 15, [w, ht])
                    dst, base = ((qT[g], 1536), (kT[g], KBASE[g]), (vT[g], KBASE[g]))[kind]
                    act(dst, dst.t[:, t0 - base:t0 - base + 512], b.t[:, :], AF.Copy, [b])
            for g in range(3):
                dil = DILS[g]
                hd = g * 4 + s
                units = []
                if g == 0:
                    units.append((0, 15, 126, 128, True))
                    units += [(0, n, 0, 128, True) for n in range(16, 32)]
                elif g == 1:
                    units += [(2, 3, 127, 128, True), (3, 3, 127, 128, True)]
                    units += [(r, n, 0, 128, True) for r in range(4) for n in range(4, 8)]
                else:
                    units += [(14, 0, 127, 128, False), (15, 0, 127, 128, False)]
                    units += [(r, 1, 0, 128, True) for r in range(16)]
                vcache = {}

                def vblock(r, n):
                    if (r, n) in vcache:
                        return vcache[(r, n)]
                    st = n * 128 * dil + r - KBASE[g]
                    tp = tps.next()
                    src = vT[g].t[:, sl(st, 128, dil)]
                    P.op("pe", lambda E: E.transpose(tp.t, src, ident.t[:]), reads=[vT[g], ident], writes=[tp])
                    vb = vbr.next()
                    P.op("act", lambda E: E.activation(out=vb.t[:], in_=tp.t, func=AF.Copy), reads=[tp], writes=[vb])
                    if len(vcache) >= 3:
                        vcache.pop(next(iter(vcache)))
                    vcache[(r, n)] = vb
                    return vb

                for (r, n, qa, qb, has_prev) in units:
                    nq = qb - qa
                    qtok = (n * 128 + qa) * dil + r
                    qs = qtok - 1536
                    q_ap = qT[g].t[:, sl(qs, nq, dil)]
                    blocks = [(n, 0)] + ([(n - 1, 1)] if has_prev else [])
                    S = sring.next()
                    for (kn, which) in blocks:
                        kst = kn * 128 * dil + r - KBASE[g]
                        mm(S, S.t[:, which * 128:which * 128 + nq], kT[g].t[:, sl(kst, 128, dil)], q_ap, True, True,
                           [kT[g], qT[g]])
                    tm, Pt = tmr.next(), ptr.next()
                    ctxs = [(kn * 128 + 127) * dil + r < NOWN for (kn, which) in blocks]
                    if nq == 128 and has_prev:
                        stt(tm, tm.t[:, 0:256], S.t[:, 0:256], SC_DIL, bm.t[:, hd, :], ALU.mult, ALU.add, [S, bm])
                    else:
                        for (kn, which) in blocks:
                            c = which * 128
                            stt(tm, tm.t[:, c:c + nq], S.t[:, c:c + nq], SC_DIL, bm.t[:, hd, c + qa:c + qb], ALU.mult, ALU.add,
                                [S, bm])
                    if nq == 128 and has_prev and ctxs[0] == ctxs[1]:
                        act(Pt, Pt.t[:, 0:256], tm.t[:, 0:256], AF.Exp, [tm, vecs], bias=(vcol(V_CTX) if ctxs[0] else None))
                    else:
                        for (kn, which), cx in zip(blocks, ctxs):
                            c = which * 128
                            act(Pt, Pt.t[:, c:c + nq], tm.t[:, c:c + nq], AF.Exp, [tm, vecs], bias=(vcol(V_CTX) if cx else None))
                    vbs = [(vblock(r, kn), which) for (kn, which) in blocks]
                    us = qtok - HALO0

                    def stage_b(vbs=vbs, Pt=Pt, nq=nq, us=us, g=g, dil=dil):
                        Ob, Zb = ob.next()
                        for i, (vb, which) in enumerate(vbs):
                            mm(Ob, Ob.t[:, 0:nq], vb.t[:], Pt.t[:, which * 128:which * 128 + nq], i == 0, i == len(vbs) - 1, [vb, Pt])
                        for i, (vb, which) in enumerate(vbs):
                            mm(Zb, Zb.t[:, 0:nq], ones.t[:], Pt.t[:, which * 128:which * 128 + nq], i == 0, i == len(vbs) - 1,
                               [ones, Pt])
                        u_ap = Uacc.t[:, sl(us, nq, dil)]
                        z_ap = Zacc.t[:, sl(us, nq, dil)]
                        if g == 0:
                            P.op("dve", lambda E, o=u_ap, i=Ob.t[:, 0:nq]: E.tensor_copy(out=o, in_=i), reads=[Ob], writes=[Uacc])
                            P.op("dve", lambda E, o=z_ap, i=Zb.t[:, 0:nq]: E.tensor_copy(out=o, in_=i), reads=[Zb], writes=[Zacc])
                        else:
                            tt(Uacc, u_ap, Ob.t[:, 0:nq], u_ap, ALU.add, [Ob, Uacc])
                            tt(Zacc, z_ap, Zb.t[:, 0:nq], z_ap, ALU.add, [Zb, Zacc])
                    pend.append(stage_b)
                    while len(pend) > LOOK:
                        pend.pop(0)()
            if s < 3:
                pf = prefetch(s + 1)
            while pend:
                pend.pop(0)()
            act(Zacc, Zacc.t[:], Zacc.t[:], AF.Ln, [Zacc, vecs], bias=vcol(V_TINY))
            act(Zacc, Zacc.t[:], Zacc.t[:], AF.Exp, [Zacc], scale=-1.0)
            tt(dout, dout.t[:], Uacc.t[:], Zacc.t[:], ALU.mult, [Uacc, Zacc])
            P.dma("sp", MOs[8 + s], dout.t[:], reads=[dout], key="dout")
        P.end_phase()
    if stop_after == 3:
        with ExitStack() as ph:
            mo_dbg = sb(ph, "mo_dbg", [128, 12, NEXT], BF16)
            P.dma("sp", mo_dbg.t[:], MOs.rearrange("c p t -> p c t"), writes=[mo_dbg], key="modbg")
            dump("mo", mo_dbg, [128, 12 * NEXT])
            P.end_phase()
        return nc, es, P

    with ExitStack() as ph:
        NC = 514
        xs = sb(ph, "x4", [128, 16, NC], F32)
        hs = sb(ph, "h4", [128, 16, NC], BF16)
        hsc = [T("h4c%d" % k, None) for k in range(16)]
        sqm = sb(ph, "sqm4", [128, 16, NC], BF16)
        mo = sb(ph, "mo4", [128, 12, NC], BF16)
        hid = sb(ph, "hid4", [128, 22, 512], BF16)
        hidb = sb(ph, "hidb4", [128, 2, 512], BF16)
        wr = Ring([sb(ph, "w4_%d" % i, [128, 16, 128], BF16) for i in range(8)])
        wdr = Ring([sb(ph, "wd4_%d" % i, [128, 22, 128], BF16) for i in range(3)])
        sgr = Ring([sb(ph, "sg4_%d" % i, [128, NC], BF16) for i in range(4)])
        tmr = Ring([sb(ph, "tm4_%d" % i, [128, NC], F32) for i in range(4)])
        u0r = Ring([sb(ph, "u04_%d" % i, [128, NC], F32) for i in range(4)])
        acr = Ring([sb(ph, "ac4_%d" % i, [128, 512], F32) for i in range(4)])
        rsr = Ring([sb(ph, "rs4_%d" % i, [128, NC], F32) for i in range(2)])
        carry = sb(ph, "carry4", [128, 86, 2], F32)
        pb = Ring(banks[0:5])
        wc = [0, 0]

        def wload(dram_ap, kcn):
            w = wr.next()
            P.dma("pool", w.t[:, 0:kcn, :], dram_ap.rearrange("p (k m) -> p k m", m=128), writes=[w],
                  key="w4_%d" % (wc[0] % 8))
            wc[0] += 1
            return w

        sqr = Ring([sb(ph, "sq4_%d" % i, [128, NC], BF16) for i in range(3)])
        ss_main, ss_halo = banks[5], banks[6]

        def load_h_mo(it):
            jt = (NOWN + it * 512) // 256
            P.dma("sp", hs.t[:, :, 2:258], Hs[jt], writes=hsc, key="h4", par=True)
            P.dma("sp", hs.t[:, :, 258:NC], Hs[jt + 1], writes=hsc, key="h4", par=True)
            if it == 0:
                P.dma("sp", hs.t[:, :, 0:2], Hs[7][:, :, 254:256], writes=hsc, key="h4", par=True)
            P.dma("sp", mo.t[:, :, 2:NC], MOs[:, :, 2 + it * 512:2 + (it + 1) * 512].rearrange("c p t -> p c t"),
                  writes=[mo], key="mo4")
            if it == 0:
                P.dma("sp", mo.t[:, :, 0:2], MOs[:, :, 0:2].rearrange("c p t -> p c t"), writes=[mo], key="mo4")

        def load_x(it):
            jt = (NOWN + it * 512) // 256
            P.dma("sp", xs.t[:, :, 2:258], xT[jt], writes=[xs], key="x4")
            P.dma("sp", xs.t[:, :, 258:NC], xT[jt + 1], writes=[xs], key="x4")
            if it == 0:
                P.dma("sp", xs.t[:, :, 0:2], xT[7][:, :, 254:256], writes=[xs], key="x4")

        load_h_mo(0)
        load_x(0)
        for it in range(4):
            segs = ([(0, 2)] if it == 0 else []) + [(2, 512)]
            lo = segs[0][0]

            def linear(w, kcn, rhs_T, rhs_of_kc):
                outs = []
                for (c, n) in segs:
                    b = pb.next()
                    for kc in range(kcn):
                        rT = rhs_T[kc] if isinstance(rhs_T, list) else rhs_T
                        mm(b, b.t[:, 0:n], w.t[:, kc, :], rhs_of_kc(kc, c, n), kc == 0, kc == kcn - 1, [w, rT])
                    outs.append((b, c, n))
                return outs

            def sumsq_then(m, lagq, seglist):
                sq = sqr.next()
                c_lo = seglist[0][0]
                act(sq, sq.t[:, c_lo:NC], xs.t[:, m, c_lo:NC], AF.Square, [xs])

                def emit(m=m, sq=sq):
                    for (c, n) in seglist:
                        sbk = ss_halo if n == 2 else ss_main
                        mm(sbk, sbk.t[:, 0:n], ones.t[:], sq.t[:, c:c + n], m == 0, m == 15, [ones, sq])
                lagq.append(emit)
                while len(lagq) > 1:
                    lagq.pop(0)()

            def rstd_cols(rs, seglist):
                for (c, n) in seglist:
                    sbk = ss_halo if n == 2 else ss_main
                    act(rs, rs.t[:, c:c + n], sbk.t[:, 0:n], AF.Ln, [sbk, vecs], bias=vcol(V_EPS), scale=1.0 / D)
                    act(rs, rs.t[:, c:c + n], rs.t[:, c:c + n], AF.Exp, [rs], scale=-0.5)

            for m in range(16):
                wga = wload(w_gate_d[m], 16)
                woa = wload(w_omla_d[m], 8)
                wgb = wload(w_gate_d[16 + m], 16)
                wob = wload(w_odil_d[m], 4)
                tms = tmr.next()
                for half, (wg, wo, kcn, mobase) in enumerate(((wga, woa, 8, 0), (wgb, wob, 4, 8))):
                    sg = sgr.next()
                    for (b, c, n) in linear(wg, 16, hsc, lambda kc, c, n: hs.t[:, kc, c:c + n]):
                        act(sg, sg.t[:, c:c + n], b.t[:, 0:n], AF.Sigmoid, [b, vecs], bias=vcol(V_BGATE + half * 16 + m))
                    for (b, c, n) in linear(wo, kcn, mo, lambda kc, c, n, mb=mobase: mo.t[:, mb + kc, c:c + n]):
                        if half == 0:
                            tt(tms, tms.t[:, c:c + n], b.t[:, 0:n], sg.t[:, c:c + n], ALU.mult, [b, sg])
                        else:
                            tm2 = tmr.next()
                            tt(tm2, tm2.t[:, c:c + n], b.t[:, 0:n], sg.t[:, c:c + n], ALU.mult, [b, sg])
                            tt(sqm, sqm.t[:, m, c:c + n], tms.t[:, c:c + n], tm2.t[:, c:c + n], ALU.add, [tms, tm2])
            lagq = []
            for m in range(16):
                w = wload(w_out_d[m], 16)
                for (b, c, n) in linear(w, 16, sqm, lambda kc, c, n: sqm.t[:, kc, c:c + n]):
                    tt(xs, xs.t[:, m, c:c + n], b.t[:, 0:n], xs.t[:, m, c:c + n], ALU.add, [b, xs])
                sumsq_then(m, lagq, segs)
                act(hsc[m], hs.t[:, m, lo:NC], xs.t[:, m, lo:NC], AF.Copy, [xs, vecs], scale=vcol(V_FFN_G + m))
            while lagq:
                lagq.pop(0)()
            rs = rsr.next()
            rstd_cols(rs, segs)
            for kc in range(16):
                tt(hsc[kc], hs.t[:, kc, lo:NC], hs.t[:, kc, lo:NC], rs.t[:, lo:NC], ALU.mult, [hsc[kc], rs])
            lagq = []
            def ffn_chunk(ci, dst_T, dst_slot):
                accs = []
                for which, mo_i in ((0, ci), (1, 43 + ci)):
                    w = wload(w_up_d[mo_i], 16)
                    u0 = u0r.next()
                    if it > 0:
                        act(u0, u0.t[:, 0:2], carry.t[:, mo_i, :], AF.Copy, [carry])
                    for (b, c, n) in linear(w, 16, hsc, lambda kc, c, n: hs.t[:, kc, c:c + n]):
                        act(u0, u0.t[:, c:c + n], b.t[:, 0:n], AF.Copy, [b])
                    if it < 3:
                        act(carry, carry.t[:, mo_i, :], u0.t[:, 512:514], AF.Copy, [u0])
                    ac = acr.next()
                    cw = V_CONVW
                    act(ac, ac.t[:], u0.t[:, 2:514], AF.Identity, [u0, vecs], bias=vcol(V_CONVB + mo_i),
                        scale=vcol(cw + 2 * 86 + mo_i))
                    stt(ac, ac.t[:], u0.t[:, 1:513], vcol(cw + 1 * 86 + mo_i), ac.t[:], ALU.mult, ALU.add, [u0, ac, vecs])
                    stt(ac, ac.t[:], u0.t[:, 0:512], vcol(cw + 0 * 86 + mo_i), ac.t[:], ALU.mult, ALU.add, [u0, ac, vecs])
                    accs.append(ac)
                sl_ = tmr.next()
                act(sl_, sl_.t[:, 0:512], accs[1].t[:], AF.Silu, [accs[1]])
                tt(dst_T, dst_T.t[:, dst_slot, :], sl_.t[:, 0:512], accs[0].t[:], ALU.mult, [sl_, accs[0]])

            def w_down(hf, kcs, wdn, src_of_kc, last):
                for m in range(16):
                    wd = wdr.next()
                    P.dma("pool", wd.t[:, 0:len(kcs), :],
                          wdn[m].rearrange("p (k m) -> p k m", m=128)[:, kcs[0]:kcs[-1] + 1, :], writes=[wd],
                          key="wd4_%d" % (wc[1] % 3))
                    wc[1] += 1
                    b = pb.next()
                    for i, kc in enumerate(kcs):
                        src_T, src_ap = src_of_kc(kc)
                        mm(b, b.t[:, :], wd.t[:, i, :], src_ap, i == 0, i == len(kcs) - 1, [wd, src_T])
                    tt(xs, xs.t[:, m, 2:NC], b.t[:, :], xs.t[:, m, 2:NC], ALU.add, [b, xs])
                    if last:
                        sumsq_then(m, lagq, [(2, 512)])

            for ci in range(0, 22):
                ffn_chunk(ci, hid, ci)
            for ci in range(22, 24):
                ffn_chunk(ci, hidb, ci - 22)
            w_down(0, list(range(22)), w_dn0_d, lambda kc: (hid, hid.t[:, kc, :]), False)
            for ci in range(24, 43):
                ffn_chunk(ci, hid, ci - 24)
            if it < 3:
                load_h_mo(it + 1)
            src1 = lambda kc: (hidb, hidb.t[:, kc, :]) if kc < 2 else (hid, hid.t[:, kc - 2, :])
            w_down(1, list(range(0, 19)), w_dn1_d, src1, False)
            w_down(1, list(range(19, 21)), w_dn1_d, src1, True)
            while lagq:
                lagq.pop(0)()
            rs2 = rsr.next()
            rstd_cols(rs2, [(2, 512)])
            for kc in range(16):
                stt(xs, xs.t[:, kc, 2:NC], xs.t[:, kc, 2:NC], vcol(V_FIN_G + kc), rs2.t[:, 2:NC], ALU.mult, ALU.mult,
                    [xs, rs2, vecs])
            P.dma("sp", outT[it], xs.t[:, :, 2:NC], reads=[xs], key="out4")
            if it < 3:
                load_x(it + 1)
        P.end_phase()
    return nc, es, P


def _wtile(W):
    K, M = W.shape
    return np.ascontiguousarray(W.reshape(K // 128, 128, M // 128, 128).transpose(2, 1, 0, 3)).reshape(M // 128, 128, (K // 128) * 128)


def _pvec(v):
    return np.asarray(v, np.float32).reshape(-1, 128).T


def _constants():
    tri = (np.arange(128)[:, None] <= np.arange(128)[None, :]).astype(np.float32)
    ident = np.eye(128, dtype=np.float32)
    slopes = (2.0 ** (-8.0 * np.arange(1, 13, dtype=np.float32) / 12.0)).astype(np.float32)
    k = np.arange(128, dtype=np.float32)[:, None]
    q = np.arange(128, dtype=np.float32)[None, :]
    bm = np.zeros((128, 12, 2, 128), np.float32)
    for g in range(3):
        for s in range(4):
            hd = g * 4 + s
            sl = slopes[hd] * DILS[g]
            bm[:, hd, 0, :] = np.where(k <= q, -sl * (q - k), NEG)
            bm[:, hd, 1, :] = np.where(k >= q, -sl * (q + 128.0 - k), NEG)
    return tri, ident, bm.reshape(128, -1)


def _rope_tables(half):
    pos = np.arange(SEQ, dtype=np.float64) - (0.0 if half == 1 else float(NOWN))
    pos = np.maximum(pos, 0.0)
    inv_freq = 10000.0 ** (-np.arange(0, 64, 2, dtype=np.float64) / 64.0)
    ang = pos[None, :] * inv_freq[:, None]
    c, s = np.cos(ang).astype(np.float32), np.sin(ang).astype(np.float32)
    return np.concatenate([c, c], 0), np.concatenate([-s, s], 0)


_CACHE = {}


def kernel(x, attn_norm_g, w_in, b_gate, q_norm_g, w_uq, kv_norm_g, w_ukv, w_o_mla, w_o_dil, w_out, ffn_norm_g,
           w_up, conv_w, conv_b, w_down, final_norm_g):
    f = lambda a: np.asarray(a, np.float32)
    x, w_in, w_uq, w_ukv = f(x), f(w_in)[0], f(w_uq)[0], f(w_ukv)[0]
    swap = np.concatenate([np.arange(32, 64), np.arange(0, 32)])
    w_lat = _wtile(np.concatenate([w_in[:, 0:832], w_in[:, 768:832][:, swap]], 1))
    w_dil = _wtile(w_in[:, 832:5440])
    w_gate = _wtile(w_in[:, 5440:9536])
    uq = w_uq.reshape(512, 8, 192)
    uq_ext = np.concatenate([uq[:, :, 0:128], uq[:, :, 128:192], uq[:, :, 128:192][:, :, swap]], 2).reshape(512, 8 * 256)
    shared = {
        "w_lat": w_lat, "w_dil": w_dil, "w_gate": w_gate, "w_uq": _wtile(uq_ext), "w_ukv": _wtile(w_ukv),
        "w_omla": _wtile(f(w_o_mla)[0]), "w_odil": _wtile(f(w_o_dil)[0]), "w_out": _wtile(f(w_out)[0]),
        "w_up": _wtile(f(w_up)[0]), "w_dn0": _wtile(f(w_down)[0][0:2816]), "w_dn1": _wtile(f(w_down)[0][2816:]),
    }
    tri, ident, bm = _constants()
    shared.update({"tri": tri, "ident": ident, "biasmat": bm})
    vec_common = np.concatenate([
        _pvec(f(attn_norm_g)[0]), _pvec(f(ffn_norm_g)[0]), _pvec(f(final_norm_g)), _pvec(f(q_norm_g)[0]),
        _pvec(f(kv_norm_g)[0]), _pvec(f(b_gate)[0]), _pvec(f(conv_b)[0]),
        f(conv_w)[0].reshape(3, 86, 128).transpose(2, 0, 1).reshape(128, 258)], 1)
    in_maps = []
    for c in range(8):
        b, half = c // 2, c % 2
        if half == 1:
            xl = x[b]
        else:
            xl = np.concatenate([np.zeros((NOWN, D), np.float32), x[b, :NOWN]], 0)
        xt = np.ascontiguousarray(xl.reshape(16, 256, 16, 128).transpose(0, 3, 2, 1))
        extra = np.zeros((128, 3), np.float32)
        extra[:, 0] = 0.0 if half == 1 else NEG
        extra[:, 1] = EPS
        extra[:, 2] = 1e-18
        cosT, sinT = _rope_tables(half)
        m = dict(shared)
        m.update({"xT": xt, "vecs": np.ascontiguousarray(np.concatenate([vec_common, extra], 1)), "cosT": cosT, "sinT": sinT})
        in_maps.append(m)
    if _CACHE.get("maps_only"):
        return in_maps
    if "nc" not in _CACHE:
        _CACHE["nc"] = build_program()
    nc = _CACHE["nc"][0]
    res = run_bass_kernel_spmd(nc, in_maps, core_ids=list(range(8)))
    out = np.empty((4, SEQ, D), np.float32)
    for c in range(8):
        b, half = c // 2, c % 2
        o = res.results[c]["outT"]
        out[b, half * NOWN:(half + 1) * NOWN] = o.transpose(0, 3, 2, 1).reshape(NOWN, D)
    return out
```

```python
import numpy as np
from contextlib import ExitStack
import concourse.bass as bass
import concourse.mybir as mybir
from concourse.bass_utils import run_bass_kernel_spmd

F32, BF16 = mybir.dt.float32, mybir.dt.bfloat16
AF = mybir.ActivationFunctionType
ALU = mybir.AluOpType

D = 2048
SEQ = 4096
NOWN = 2048
HALO0 = 2046
NEXT = NOWN + 2
D_FF = 5504
EPS = 1e-6
NEG = -30000.0
SC_MLA = 192.0 ** -0.5
SC_DIL = 128.0 ** -0.5
DILS = (1, 4, 16)

V_ATTN_G, V_FFN_G, V_FIN_G, V_Q_G, V_KV_G, V_BGATE, V_CONVB, V_CONVW, V_CTX, V_EPS, V_TINY, NVEC = \
    0, 16, 32, 48, 52, 54, 86, 172, 430, 431, 432, 433


DBG = {}


class T:
    __slots__ = ("name", "t", "lw", "rd")

    def __init__(self, name, t):
        self.name, self.t, self.lw, self.rd = name, t, None, []


class View:
    def __init__(self, parent, ap):
        self.p, self.t = parent, ap
    lw = property(lambda s: s.p.lw, lambda s, v: setattr(s.p, "lw", v))
    rd = property(lambda s: s.p.rd, lambda s, v: setattr(s.p, "rd", v))


class Op:
    __slots__ = ("eng", "fn", "reads", "writes", "dma", "key", "deps", "need_inc", "sem", "val",
                 "idx", "phase", "xdeps", "par")

    def __init__(self, eng, fn, reads, writes, dma, key, xdeps=None):
        self.eng, self.fn, self.reads, self.writes, self.dma, self.key = eng, fn, reads, writes, dma, key
        self.deps, self.need_inc, self.sem, self.val, self.xdeps = [], False, None, 0, xdeps or []
        self.par = False


class Prog:
    ENGS = ("pe", "act", "dve", "pool", "sp")

    def __init__(self, nc, es):
        self.nc, self.es = nc, es
        self.eobj = {"pe": nc.tensor, "act": nc.scalar, "dve": nc.vector, "pool": nc.gpsimd, "sp": nc.sync}
        self.esem = {e: es.enter_context(nc.semaphore("sem_" + e)) for e in self.ENGS}
        self.ecnt = {e: 0 for e in self.ENGS}
        self.ksem, self.kcnt = {}, {}
        self.waited = {e: {} for e in self.ENGS}
        self.ops, self.nops, self.phase = [], 0, 0
        self.last = {e: None for e in self.ENGS}
        self.pending_dma = []
        self.stores = []
        self.stats = {e: 0 for e in self.ENGS}

    def op(self, eng, fn, reads=(), writes=(), dma=False, key=None, xdeps=None):
        o = Op(eng, fn, list(reads), list(writes), dma, key, xdeps)
        o.idx, o.phase = self.nops, self.phase
        self.nops += 1
        self.ops.append(o)
        return o

    def dma(self, eng, out_ap, in_ap, reads=(), writes=(), key=None, par=False, **kw):
        assert key is not None
        o = self.op(eng, lambda E: E.dma_start(out=out_ap, in_=in_ap, **kw), reads, writes, True, key)
        o.par = par
        return o

    def _sem_for_key(self, key):
        if key not in self.ksem:
            self.ksem[key] = self.es.enter_context(self.nc.semaphore("k_" + key))
            self.kcnt[key] = 0
        return self.ksem[key]

    def end_phase(self):
        ops = self.ops
        for o in ops:
            deps = set(o.xdeps)
            for t in o.reads:
                if t.lw is not None:
                    deps.add(t.lw)
                t.rd.append(o)
            for t in o.writes:
                if t.lw is not None:
                    deps.add(t.lw)
                for r in t.rd:
                    deps.add(r)
                t.rd = []
                t.lw = o
            deps.discard(o)
            keep, best = [], {}
            for d in deps:
                if d.phase < o.phase:
                    continue
                if d.dma:
                    if o.dma and o.par and d.key == o.key and d.eng == o.eng:
                        continue
                    keep.append(d)
                elif d.eng == "pe" and o.eng == "pe" and not o.dma:
                    continue
                else:
                    b = best.get(d.eng)
                    if b is None or d.idx > b.idx:
                        best[d.eng] = d
            keep.extend(best.values())
            o.deps = keep
            for d in keep:
                d.need_inc = True
            if o.dma:
                self.pending_dma.append(o)
            self.last[o.eng] = o
        bar_deps = [x for x in self.last.values() if x is not None and x.phase == self.phase and not x.dma]
        for d in bar_deps:
            d.need_inc = True
        bar_deps = bar_deps + [d for d in self.pending_dma if d.phase == self.phase]
        bars = []
        for e in self.ENGS:
            b = Op(e, None, [], [], False, None)
            b.idx, b.phase, b.deps = self.nops, self.phase, bar_deps
            self.nops += 1
            bars.append(b)
        for o in ops + bars:
            E = self.eobj[o.eng]
            w = self.waited[o.eng]
            need = {}
            for d in o.deps:
                sid = id(d.sem)
                if sid not in need or need[sid][1] < d.val:
                    need[sid] = (d.sem, d.val)
            for sid, (sem, val) in need.items():
                if w.get(sid, 0) < val:
                    E.wait_ge(sem, val)
                    w[sid] = val
            if o.fn is None:
                continue
            if o.dma:
                o.sem = self._sem_for_key(o.key)
                self.kcnt[o.key] += 16
                o.val = self.kcnt[o.key]
                ins = o.fn(E)
                ins.then_inc(o.sem, 16)
            else:
                ins = o.fn(E)
                if o.need_inc:
                    self.ecnt[o.eng] += 1
                    o.sem, o.val = self.esem[o.eng], self.ecnt[o.eng]
                    ins.then_inc(o.sem, 1)
            self.stats[o.eng] += 1
        self.ops = []
        self.pending_dma = []
        self.phase += 1


class Ring:
    def __init__(self, tiles):
        self.tiles, self.i = tiles, 0

    def next(self):
        t = self.tiles[self.i % len(self.tiles)]
        self.i += 1
        return t


def build_program(stop_after=99, dbg=False):
    nc = bass.Bass("TRN2", target_bir_lowering=False)
    es = ExitStack()
    P = Prog(nc, es)

    def dram(name, shape, dt, kind):
        return nc.dram_tensor(name, list(shape), dt, kind=kind).ap()

    xT = dram("xT", [16, 128, 16, 256], F32, "ExternalInput")
    vecs_d = dram("vecs", [128, NVEC], F32, "ExternalInput")
    cos_d = dram("cosT", [64, SEQ], F32, "ExternalInput")
    sin_d = dram("sinT", [64, SEQ], F32, "ExternalInput")
    tri_d = dram("tri", [128, 128], F32, "ExternalInput")
    bm_d = dram("biasmat", [128, 12 * 2 * 128], F32, "ExternalInput")
    ident_d = dram("ident", [128, 128], F32, "ExternalInput")
    w_lat_d = dram("w_lat", [7, 128, 16 * 128], F32, "ExternalInput")
    w_dil_d = dram("w_dil", [36, 128, 16 * 128], F32, "ExternalInput")
    w_gate_d = dram("w_gate", [32, 128, 16 * 128], F32, "ExternalInput")
    w_uq_d = dram("w_uq", [16, 128, 4 * 128], F32, "ExternalInput")
    w_ukv_d = dram("w_ukv", [16, 128, 2 * 128], F32, "ExternalInput")
    w_omla_d = dram("w_omla", [16, 128, 8 * 128], F32, "ExternalInput")
    w_odil_d = dram("w_odil", [16, 128, 4 * 128], F32, "ExternalInput")
    w_out_d = dram("w_out", [16, 128, 16 * 128], F32, "ExternalInput")
    w_up_d = dram("w_up", [86, 128, 16 * 128], F32, "ExternalInput")
    w_dn0_d = dram("w_dn0", [16, 128, 22 * 128], F32, "ExternalInput")
    w_dn1_d = dram("w_dn1", [16, 128, 21 * 128], F32, "ExternalInput")
    outT = dram("outT", [4, 128, 16, 512], F32, "ExternalOutput")
    Hs = dram("Hs", [16, 128, 16, 256], BF16, "Internal")
    MOs = dram("MOs", [12, 128, NEXT], BF16, "Internal")
    Hs_T = T("Hs", None)
    MOs_T = T("MOs", None)

    def sb(ctx, name, shape, dt):
        return T(name, ctx.enter_context(nc.sbuf_tensor(name, list(shape), dt)))

    banks = [T("bank%d" % i, es.enter_context(nc.psum_tensor("bank%d" % i, [128, 512], F32))) for i in range(7)]
    tbank_t = es.enter_context(nc.psum_tensor("tbank", [128, 1024], BF16))
    tbank = T("tbank", tbank_t)
    tps = Ring([View(tbank, tbank_t[:, i * 128:(i + 1) * 128]) for i in range(8)])

    vecs = sb(es, "vecs_sb", [128, NVEC], F32)
    ones = sb(es, "ones_sb", [128, 128], BF16)
    tri = sb(es, "tri_sb", [128, 128], BF16)
    ident = sb(es, "ident_sb", [128, 128], BF16)
    P.dma("sp", vecs.t[:], vecs_d, writes=[vecs], key="c_vecs")
    P.dma("pool", tri.t[:], tri_d, writes=[tri], key="c_tri")
    P.dma("pool", ident.t[:], ident_d, writes=[ident], key="c_ident")
    P.op("dve", lambda E: E.memset(ones.t[:], 1.0), writes=[ones])
    ones32 = sb(es, "ones32_sb", [128, 128], F32)
    P.op("dve", lambda E: E.memset(ones32.t[:], 1.0), writes=[ones32])

    def dump(name, tile, shape2):
        d = dram("dbg_" + name, shape2, F32, "ExternalOutput")
        src = tile.t[:]
        if len(src.shape) == 3:
            d = d.rearrange("p (a b) -> p a b", a=src.shape[1])
        elif len(src.shape) == 4:
            d = d.rearrange("p (a b c) -> p a b c", a=src.shape[1], b=src.shape[2])
        P.dma("pool", d, src, reads=[tile], key="dbg_" + name)

    def sl(start, count, step):
        return slice(start, start + (count - 1) * step + 1, step)

    def vcol(c, n=128):
        return vecs.t[0:n, c:c + 1]

    def mm(out_T, out_ap, lhsT_ap, rhs_ap, start, stop, reads):
        P.op("pe", lambda E: E.matmul(out_ap, lhsT=lhsT_ap, rhs=rhs_ap, start=start, stop=stop),
             reads=reads, writes=[out_T])

    def act(out_T, out_ap, in_ap, func, reads, bias=None, scale=None):
        kw = {}
        if bias is not None:
            kw["bias"] = bias
        if scale is not None:
            kw["scale"] = scale
        P.op("act", lambda E: E.activation(out=out_ap, in_=in_ap, func=func, **kw), reads=reads, writes=[out_T])

    def tt(out_T, out_ap, in0, in1, op, reads, eng="dve"):
        P.op(eng, lambda E: E.tensor_tensor(out=out_ap, in0=in0, in1=in1, op=op), reads=reads, writes=[out_T])

    def stt(out_T, out_ap, in0, scalar, in1, op0, op1, reads):
        P.op("dve", lambda E: E.scalar_tensor_tensor(out=out_ap, in0=in0, scalar=scalar, in1=in1, op0=op0, op1=op1),
             reads=reads, writes=[out_T])

    def ts(out_T, out_ap, in0, s1, s2, op0, op1, reads):
        if s2 is None:
            P.op("dve", lambda E: E.tensor_scalar(out=out_ap, in0=in0, scalar1=s1, scalar2=None, op0=op0),
                 reads=reads, writes=[out_T])
        else:
            P.op("dve", lambda E: E.tensor_scalar(out=out_ap, in0=in0, scalar1=s1, scalar2=s2, op0=op0, op1=op1),
                 reads=reads, writes=[out_T])

    def recip(out_T, ap, reads):
        P.op("dve", lambda E: E.reciprocal(out=ap, in_=ap), reads=reads, writes=[out_T])

    def rstd_from_ss(ss_T, n, rs_T, inv_dim):
        act(rs_T, rs_T.t[:, 0:n], ss_T.t[:, 0:n], AF.Ln, [ss_T, vecs], bias=vcol(V_EPS), scale=inv_dim)
        act(rs_T, rs_T.t[:, 0:n], rs_T.t[:, 0:n], AF.Exp, [rs_T], scale=-0.5)

    TW = 256
    lat = ExitStack()
    cqn = sb(lat, "cqn", [128, 4, NEXT], BF16)
    ckvn = sb(lat, "ckvn", [128, 2, SEQ], BF16)
    kpe = sb(lat, "kpe", [64, SEQ], BF16)
    with ExitStack() as ph:
        wl = [sb(ph, "wl%d" % i, [128, 16, 128], BF16) for i in range(7)]
        for i in range(7):
            P.dma("pool", wl[i].t[:], w_lat_d[i].rearrange("p (k m) -> p k m", m=128), writes=[wl[i]], key="wl%d" % i)
        xr = Ring([sb(ph, "x1_%d" % i, [128, 16, TW], F32) for i in range(3)])
        sqr = Ring([sb(ph, "sq1_%d" % i, [128, 16, TW], BF16) for i in range(2)])
        hr = Ring([sb(ph, "h1_%d" % i, [128, 16, TW], BF16) for i in range(4)])
        rsr = Ring([sb(ph, "rs1_%d" % i, [128, TW], F32) for i in range(5)])
        raw = Ring([sb(ph, "raw1_%d" % i, [128, 4, TW], F32) for i in range(2)])
        sqs = Ring([sb(ph, "sqs1_%d" % i, [128, 4, TW], BF16) for i in range(2)])
        csr = Ring([sb(ph, "cs1_%d" % i, [64, 2, TW], F32) for i in range(4)])
        tmr = Ring([sb(ph, "tm1_%d" % i, [64, 2, TW], F32) for i in range(2)])
        pb = Ring(banks[0:5])
        ssb = Ring(banks[5:7])
        def stage_n(j):
            t0 = j * TW
            jt, jo = t0 // 512, t0 % 512
            xt = xr.next()
            P.dma("sp", xt.t[:], xT[j], writes=[xt], key="x1_%d" % (j % 3))
            cs = csr.next()
            P.dma("sp", cs.t[:, 0, :], cos_d[:, t0:t0 + TW], writes=[cs], key="cs1a_%d" % (j % 4))
            P.dma("sp", cs.t[:, 1, :], sin_d[:, t0:t0 + TW], writes=[cs], key="cs1b_%d" % (j % 4))
            sq = sqr.next()
            act(sq, sq.t[:], xt.t[:], AF.Square, [xt])
            ss = ssb.next()
            for kc in range(16):
                mm(ss, ss.t[:, 0:TW], ones.t[:], sq.t[:, kc, :], kc == 0, kc == 15, [ones, sq])
            rs = rsr.next()
            rstd_from_ss(ss, TW, rs, 1.0 / D)
            ht = hr.next()
            for kc in range(16):
                stt(ht, ht.t[:, kc, :], xt.t[:, kc, :], vcol(V_ATTN_G + kc), rs.t[:, 0:TW], ALU.mult, ALU.mult, [xt, rs, vecs])
            P.dma("pool", Hs[j], ht.t[:], reads=[ht], key="h1_%d" % (j % 4))
            return ht, cs

        def stage_l(j, ht, cs):
            t0 = j * TW

            def latent(chunks, gcol, nfeat, dst, dst_col, c0, n):
                rw, sqq = raw.next(), sqs.next()
                for ci, ch in enumerate(chunks):
                    b = pb.next()
                    for kc in range(16):
                        mm(b, b.t[:, 0:n], wl[ch].t[:, kc, :], ht.t[:, kc, c0:c0 + n], kc == 0, kc == 15, [wl[ch], ht])
                    act(rw, rw.t[:, ci, 0:n], b.t[:, 0:n], AF.Copy, [b])
                    act(sqq, sqq.t[:, ci, 0:n], b.t[:, 0:n], AF.Square, [b])
                s2 = ssb.next()
                for ci in range(len(chunks)):
                    mm(s2, s2.t[:, 0:n], ones.t[:], sqq.t[:, ci, 0:n], ci == 0, ci == len(chunks) - 1, [ones, sqq])
                r2 = rsr.next()
                rstd_from_ss(s2, n, r2, 1.0 / nfeat)
                for ci in range(len(chunks)):
                    stt(dst, dst.t[:, ci, dst_col:dst_col + n], rw.t[:, ci, 0:n], vcol(gcol + ci), r2.t[:, 0:n],
                        ALU.mult, ALU.mult, [rw, r2, vecs])

            if t0 + TW > HALO0:
                c0 = max(HALO0 - t0, 0)
                latent([0, 1, 2, 3], V_Q_G, 512, cqn, t0 + c0 - HALO0, c0, TW - c0)
            latent([4, 5], V_KV_G, 256, ckvn, t0, 0, TW)
            ba, bb = pb.next(), pb.next()
            for kc in range(16):
                mm(ba, ba.t[0:64, 0:TW], wl[6].t[:, kc, 0:64], ht.t[:, kc, :], kc == 0, kc == 15, [wl[6], ht])
            for kc in range(16):
                mm(bb, bb.t[0:64, 0:TW], wl[6].t[:, kc, 64:128], ht.t[:, kc, :], kc == 0, kc == 15, [wl[6], ht])
            tm = tmr.next()
            tt(tm, tm.t[:, 0, :], ba.t[0:64, 0:TW], cs.t[:, 0, :], ALU.mult, [ba, cs])
            tt(tm, tm.t[:, 1, :], bb.t[0:64, 0:TW], cs.t[:, 1, :], ALU.mult, [bb, cs])
            tt(kpe, kpe.t[:, t0:t0 + TW], tm.t[:, 0, :], tm.t[:, 1, :], ALU.add, [tm])
        NT1 = SEQ // TW
        LOOK1 = 2
        staged = [stage_n(j) for j in range(LOOK1)]
        for j in range(NT1):
            if j + LOOK1 < NT1:
                staged.append(stage_n(j + LOOK1))
            stage_l(j, *staged.pop(0))
        if dbg and stop_after == 1:
            dump("cqn", cqn, [128, 4 * NEXT])
            dump("ckvn", ckvn, [128, 2 * SEQ])
            dump("kpe", kpe, [64, SEQ])
        P.end_phase()
    if stop_after == 1:
        return nc, es, P

    QG = [(HALO0, 2)] + [(NOWN + 512 * i, 512) for i in range(4)]
    with ExitStack() as ph:
        wuq_all = sb(ph, "wuq", [128, 16, 4, 128], BF16)
        wukv_all = sb(ph, "wukv", [128, 16, 2, 128], BF16)
        P.dma("pool", wukv_all.t[:], w_ukv_d.rearrange("o p (k m) -> p o k m", m=128), writes=[wukv_all], key="wukv")
        P.dma("pool", wuq_all.t[:], w_uq_d.rearrange("o p (k m) -> p o k m", m=128), writes=[wuq_all], key="wuq")

        wuq = [View(wuq_all, wuq_all.t[:, i]) for i in range(16)]
        wukv = [View(wukv_all, wukv_all.t[:, i]) for i in range(16)]
        cosq = sb(ph, "cosq", [64, NEXT], F32)
        sinq = sb(ph, "sinq", [64, NEXT], F32)
        P.dma("sp", cosq.t[:], cos_d[:, HALO0:SEQ], writes=[cosq], key="cosq")
        P.dma("sp", sinq.t[:], sin_d[:, HALO0:SEQ], writes=[sinq], key="sinq")
        khr = Ring([sb(ph, "kh%d" % i, [128, SEQ], BF16) for i in range(2)])
        vhr = Ring([sb(ph, "vh%d" % i, [128, SEQ], BF16) for i in range(2)])
        qnr = Ring([sb(ph, "qn%d" % i, [128, NEXT], BF16) for i in range(2)])
        qpr = Ring([sb(ph, "qp%d" % i, [64, NEXT], BF16) for i in range(2)])
        ptr = Ring([sb(ph, "pt%d" % i, [128, 512], BF16) for i in range(8)])
        tmr = Ring([sb(ph, "tm2_%d" % i, [64, 2, 512], F32) for i in range(2)])
        rzr = Ring([sb(ph, "rz%d" % i, [128, 512], F32) for i in range(2)])
        osr = Ring([sb(ph, "os%d" % i, [128, 512], BF16) for i in range(2)])
        pb = Ring(banks[0:3])
        ob = Ring([(banks[3], banks[4]), (banks[5], banks[6])])
        pend, LOOK = [], 6
        for h in range(8):
            wk, wv, wqn, wqp = wukv[2 * h], wukv[2 * h + 1], wuq[2 * h], wuq[2 * h + 1]
            Kh, Vh, qn, qp = khr.next(), vhr.next(), qnr.next(), qpr.next()
            for tg in range(8):
                b = pb.next()
                for kc in range(2):
                    mm(b, b.t[:, :], wk.t[:, kc, :], ckvn.t[:, kc, tg * 512:(tg + 1) * 512], kc == 0, kc == 1, [wk, ckvn])
                act(Kh, Kh.t[:, tg * 512:(tg + 1) * 512], b.t[:, :], AF.Copy, [b])
            for tg in range(8):
                b = pb.next()
                for i in range(4):
                    tb = tg * 4 + i
                    for kc in range(2):
                        mm(b, b.t[:, i * 128:(i + 1) * 128], ckvn.t[:, kc, tb * 128:(tb + 1) * 128], wv.t[:, kc, :],
                           kc == 0, kc == 1, [wv, ckvn])
                act(Vh, Vh.t[:, tg * 512:(tg + 1) * 512], b.t[:, :], AF.Copy, [b])
            for (q0, n) in QG:
                c0 = q0 - HALO0
                b = pb.next()
                for kc in range(4):
                    mm(b, b.t[:, 0:n], wqn.t[:, kc, :], cqn.t[:, kc, c0:c0 + n], kc == 0, kc == 3, [wqn, cqn])
                act(qn, qn.t[:, c0:c0 + n], b.t[:, 0:n], AF.Copy, [b])
                ba, bb = pb.next(), pb.next()
                for kc in range(4):
                    mm(ba, ba.t[0:64, 0:n], wqp.t[:, kc, 0:64], cqn.t[:, kc, c0:c0 + n], kc == 0, kc == 3, [wqp, cqn])
                for kc in range(4):
                    mm(bb, bb.t[0:64, 0:n], wqp.t[:, kc, 64:128], cqn.t[:, kc, c0:c0 + n], kc == 0, kc == 3, [wqp, cqn])
                tm = tmr.next()
                tt(tm, tm.t[:, 0, 0:n], ba.t[0:64, 0:n], cosq.t[:, c0:c0 + n], ALU.mult, [ba, cosq])
                tt(tm, tm.t[:, 1, 0:n], bb.t[0:64, 0:n], sinq.t[:, c0:c0 + n], ALU.mult, [bb, sinq])
                tt(qp, qp.t[:, c0:c0 + n], tm.t[:, 0, 0:n], tm.t[:, 1, 0:n], ALU.add, [tm])
            for (q0, n) in QG:
                kb_last = (q0 + n - 1) // 128
                Ob, Zb = ob.next()
                for kb in range(kb_last + 1):
                    k0 = kb * 128
                    qlo = max(q0, k0)
                    nn = q0 + n - qlo
                    cl = qlo - HALO0
                    S = pb.next()
                    mm(S, S.t[:, 0:nn], Kh.t[:, k0:k0 + 128], qn.t[:, cl:cl + nn], True, False, [Kh, qn])
                    mm(S, S.t[:, 0:nn], kpe.t[0:64, k0:k0 + 128], qp.t[0:64, cl:cl + nn], False, True, [kpe, qp])
                    Pt = ptr.next()
                    act(Pt, Pt.t[:, 0:nn], S.t[:, 0:nn], AF.Exp, [S, vecs],
                        bias=(vcol(V_CTX) if k0 < NOWN else None), scale=SC_MLA)
                    if qlo < k0 + 128:
                        m = min(q0 + n, k0 + 128) - qlo
                        off = qlo - k0
                        tt(Pt, Pt.t[:, 0:m], Pt.t[:, 0:m], tri.t[:, off:off + m], ALU.mult, [Pt, tri])

                    def stage_b(Ob=Ob, Zb=Zb, Vh=Vh, Pt=Pt, k0=k0, qlo=qlo, q0=q0, n=n, nn=nn, kb=kb, kb_last=kb_last, h=h):
                        mm(Ob, Ob.t[:, qlo - q0:n], Vh.t[:, k0:k0 + 128], Pt.t[:, 0:nn], kb == 0, kb == kb_last, [Vh, Pt])
                        mm(Zb, Zb.t[:, qlo - q0:n], ones.t[:], Pt.t[:, 0:nn], kb == 0, kb == kb_last, [ones, Pt])
                        if kb == kb_last:
                            rz, osb = rzr.next(), osr.next()
                            act(rz, rz.t[:, 0:n], Zb.t[:, 0:n], AF.Ln, [Zb, vecs], bias=vcol(V_TINY))
                            act(rz, rz.t[:, 0:n], rz.t[:, 0:n], AF.Exp, [rz], scale=-1.0)
                            tt(osb, osb.t[:, 0:n], Ob.t[:, 0:n], rz.t[:, 0:n], ALU.mult, [Ob, rz])
                            P.dma("sp", MOs[h][:, q0 - HALO0:q0 - HALO0 + n], osb.t[:, 0:n], reads=[osb],
                                  key="os%d" % ((osr.i - 1) % 2))
                    pend.append(stage_b)
                    while len(pend) > LOOK:
                        pend.pop(0)()
        while pend:
            pend.pop(0)()
        P.end_phase()
    if stop_after == 2:
        with ExitStack() as ph:
            mo_dbg = sb(ph, "mo_dbg", [128, 12, NEXT], BF16)
            P.dma("sp", mo_dbg.t[:], MOs.rearrange("c p t -> p c t"), writes=[mo_dbg], key="modbg")
            dump("mo", mo_dbg, [128, 12 * NEXT])
            P.end_phase()
        return nc, es, P

    lat.close()
    with ExitStack() as ph:
        bm = sb(ph, "bm", [128, 12, 256], F32)
        hr = Ring([sb(ph, "h3_%d" % i, [128, 2, 16, 256], BF16) for i in range(2)])
        wr = Ring([sb(ph, "w3_%d" % i, [128, 16, 128], BF16) for i in range(12)])
        qT = [sb(ph, "dq%d" % g, [128, 2560], BF16) for g in range(3)]
        KBASE = (1536, 1024, 0)
        kT = [sb(ph, "dk%d" % g, [128, SEQ - KBASE[g]], BF16) for g in range(3)]
        vT = [sb(ph, "dv%d" % g, [128, SEQ - KBASE[g]], BF16) for g in range(3)]
        Uacc = sb(ph, "uacc", [128, NEXT], F32)
        Zacc = sb(ph, "zacc", [128, NEXT], F32)
        dout = sb(ph, "dout", [128, NEXT], BF16)
        vbr = Ring([sb(ph, "vb%d" % i, [128, 128], BF16) for i in range(24)])
        ptr = Ring([sb(ph, "pt3_%d" % i, [128, 256], BF16) for i in range(10)])
        tmr = Ring([sb(ph, "tm3_%d" % i, [128, 256], F32) for i in range(4)])
        pb = Ring(banks[0:4])
        sring = Ring(banks[0:3])
        ob = Ring([(banks[3], banks[4]), (banks[5], banks[6])])
        pend, LOOK = [], 6
        wcnt = [0]
        def prefetch(s):
            wt = {}
            for kind in range(3):
                for g in range(3):
                    w = wr.next()
                    mo = kind * 12 + g * 4 + s
                    P.dma("pool", w.t[:], w_dil_d[mo].rearrange("p (k m) -> p k m", m=128), writes=[w],
                          key="w3_%d" % (wcnt[0] % 12))
                    wcnt[0] += 1
                    wt[(kind, g)] = w
            hts = [load_h(0), load_h(1)]
            return wt, hts

        def load_h(j):
            ht = hr.next()
            P.dma("sp", ht.t[:], Hs[2 * j:2 * j + 2].rearrange("a p k t -> p a k t"), writes=[ht], key="h3_%d" % ((hr.i - 1) % 2))
            return ht

        pf = prefetch(0)
        P.dma("sp", bm.t[:], bm_d.rearrange("p (h c) -> p h c", h=12), writes=[bm], key="bm")
        for s in range(4):
            wt, hts = pf
            for j in range(8):
                t0 = j * 512
                need = [(kind, g) for g in range(3) for kind in (1, 2) if t0 >= KBASE[g]]
                if t0 >= 1536:
                    need += [(0, g) for g in range(3)]
                ht = hts[j] if j < 2 else load_h(j)
                for (kind, g) in need:
                    w = wt[(kind, g)]
                    b = pb.next()
                    dst, base = ((qT[g], 1536), (kT[g], KBASE[g]), (vT[g], KBASE[g]))[kind]
                    if j == 3 and kind == 0:
                        for kc in range(16):
                            mm(b, b.t[:, 0:2], w.t[:, kc, :], ht.t[:, 1, kc, 254:256], kc == 0, kc == 15, [w, ht])
                        act(dst, dst.t[:, t0 - base + 510:t0 - base + 512], b.t[:, 0:2], AF.Copy, [b])
                    elif j == 3 and g == 0:
                        for kc in range(16):
                            mm(b, b.t[:, 0:256], w.t[:, kc, :], ht.t[:, 1, kc, :], kc == 0, kc == 15, [w, ht])
                        act(dst, dst.t[:, t0 - base + 256:t0 - base + 512], b.t[:, 0:256], AF.Copy, [b])
                    else:
                        for kc in range(16):
                            mm(b, b.t[:, :], w.t[:, kc, :], ht.t[:, :, kc, :], kc == 0, kc == 15, [w, ht])
                        act(dst, dst.t[:, t0 - base:t0 - base + 512], b.t[:, :], AF.Copy, [b])
            for g in range(3):
                dil = DILS[g]
                hd = g * 4 + s
                units = []
                if g == 0:
                    units.append((0, 15, 126, 128, True))
                    units += [(0, n, 0, 128, True) for n in range(16, 32)]
                elif g == 1:
                    units += [(2, 3, 127, 128, True), (3, 3, 127, 128, True)]
                    units += [(r, n, 0, 128, True) for r in range(4) for n in range(4, 8)]
                else:
                    units += [(14, 0, 127, 128, False), (15, 0, 127, 128, False)]
                    units += [(r, 1, 0, 128, True) for r in range(16)]
                vcache = {}

                def vblock(r, n):
                    if (r, n) in vcache:
                        return vcache[(r, n)]
                    st = n * 128 * dil + r - KBASE[g]
                    tp = tps.next()
                    src = vT[g].t[:, sl(st, 128, dil)]
                    P.op("pe", lambda E: E.transpose(tp.t, src, ident.t[:]), reads=[vT[g], ident], writes=[tp])
                    vb = vbr.next()
                    P.op("act", lambda E: E.activation(out=vb.t[:], in_=tp.t, func=AF.Copy), reads=[tp], writes=[vb])
                    if len(vcache) >= 3:
                        vcache.pop(next(iter(vcache)))
                    vcache[(r, n)] = vb
                    return vb

                for (r, n, qa, qb, has_prev) in units:
                    nq = qb - qa
                    qtok = (n * 128 + qa) * dil + r
                    qs = qtok - 1536
                    q_ap = qT[g].t[:, sl(qs, nq, dil)]
                    blocks = [(n, 0)] + ([(n - 1, 1)] if has_prev else [])
                    S = sring.next()
                    for (kn, which) in blocks:
                        kst = kn * 128 * dil + r - KBASE[g]
                        mm(S, S.t[:, which * 128:which * 128 + nq], kT[g].t[:, sl(kst, 128, dil)], q_ap, True, True,
                           [kT[g], qT[g]])
                    tm, Pt = tmr.next(), ptr.next()
                    ctxs = [(kn * 128 + 127) * dil + r < NOWN for (kn, which) in blocks]
                    if nq == 128 and has_prev:
                        stt(tm, tm.t[:, 0:256], S.t[:, 0:256], SC_DIL, bm.t[:, hd, :], ALU.mult, ALU.add, [S, bm])
                    else:
                        for (kn, which) in blocks:
                            c = which * 128
                            stt(tm, tm.t[:, c:c + nq], S.t[:, c:c + nq], SC_DIL, bm.t[:, hd, c + qa:c + qb], ALU.mult, ALU.add,
                                [S, bm])
                    if nq == 128 and has_prev and ctxs[0] == ctxs[1]:
                        act(Pt, Pt.t[:, 0:256], tm.t[:, 0:256], AF.Exp, [tm, vecs], bias=(vcol(V_CTX) if ctxs[0] else None))
                    else:
                        for (kn, which), cx in zip(blocks, ctxs):
                            c = which * 128
                            act(Pt, Pt.t[:, c:c + nq], tm.t[:, c:c + nq], AF.Exp, [tm, vecs], bias=(vcol(V_CTX) if cx else None))
                    vbs = [(vblock(r, kn), which) for (kn, which) in blocks]
                    us = qtok - HALO0

                    def stage_b(vbs=vbs, Pt=Pt, nq=nq, us=us, g=g, dil=dil):
                        Ob, Zb = ob.next()
                        for i, (vb, which) in enumerate(vbs):
                            mm(Ob, Ob.t[:, 0:nq], vb.t[:], Pt.t[:, which * 128:which * 128 + nq], i == 0, i == len(vbs) - 1, [vb, Pt])
                        for i, (vb, which) in enumerate(vbs):
                            mm(Zb, Zb.t[:, 0:nq], ones.t[:], Pt.t[:, which * 128:which * 128 + nq], i == 0, i == len(vbs) - 1,
                               [ones, Pt])
                        u_ap = Uacc.t[:, sl(us, nq, dil)]
                        z_ap = Zacc.t[:, sl(us, nq, dil)]
                        if g == 0:
                            P.op("dve", lambda E, o=u_ap, i=Ob.t[:, 0:nq]: E.tensor_copy(out=o, in_=i), reads=[Ob], writes=[Uacc])
                            P.op("dve", lambda E, o=z_ap, i=Zb.t[:, 0:nq]: E.tensor_copy(out=o, in_=i), reads=[Zb], writes=[Zacc])
                        else:
                            tt(Uacc, u_ap, Ob.t[:, 0:nq], u_ap, ALU.add, [Ob, Uacc])
                            tt(Zacc, z_ap, Zb.t[:, 0:nq], z_ap, ALU.add, [Zb, Zacc])
                    pend.append(stage_b)
                    while len(pend) > LOOK:
                        pend.pop(0)()
            if s < 3:
                pf = prefetch(s + 1)
            while pend:
                pend.pop(0)()
            act(Zacc, Zacc.t[:], Zacc.t[:], AF.Ln, [Zacc, vecs], bias=vcol(V_TINY))
            act(Zacc, Zacc.t[:], Zacc.t[:], AF.Exp, [Zacc], scale=-1.0)
            tt(dout, dout.t[:], Uacc.t[:], Zacc.t[:], ALU.mult, [Uacc, Zacc])
            P.dma("sp", MOs[8 + s], dout.t[:], reads=[dout], key="dout")
        P.end_phase()
    if stop_after == 3:
        with ExitStack() as ph:
            mo_dbg = sb(ph, "mo_dbg", [128, 12, NEXT], BF16)
            P.dma("sp", mo_dbg.t[:], MOs.rearrange("c p t -> p c t"), writes=[mo_dbg], key="modbg")
            dump("mo", mo_dbg, [128, 12 * NEXT])
            P.end_phase()
        return nc, es, P

    with ExitStack() as ph:
        NC = 514
        xs = sb(ph, "x4", [128, 16, NC], F32)
        hs = sb(ph, "h4", [128, 16, NC], BF16)
        hsc = [T("h4c%d" % k, None) for k in range(16)]
        sqm = sb(ph, "sqm4", [128, 16, NC], BF16)
        mo = sb(ph, "mo4", [128, 12, NC], BF16)
        hid = sb(ph, "hid4", [128, 22, 512], BF16)
        hidb = sb(ph, "hidb4", [128, 2, 512], BF16)
        wr = Ring([sb(ph, "w4_%d" % i, [128, 16, 128], BF16) for i in range(8)])
        wdr = Ring([sb(ph, "wd4_%d" % i, [128, 22, 128], BF16) for i in range(3)])
        sgr = Ring([sb(ph, "sg4_%d" % i, [128, NC], BF16) for i in range(4)])
        tmr = Ring([sb(ph, "tm4_%d" % i, [128, NC], F32) for i in range(4)])
        u0r = Ring([sb(ph, "u04_%d" % i, [128, NC], F32) for i in range(4)])
        acr = Ring([sb(ph, "ac4_%d" % i, [128, 512], F32) for i in range(4)])
        rsr = Ring([sb(ph, "rs4_%d" % i, [128, NC], F32) for i in range(2)])
        carry = sb(ph, "carry4", [128, 86, 2], F32)
        pb = Ring(banks[0:5])
        wc = [0, 0]

        def wload(dram_ap, kcn):
            w = wr.next()
            P.dma("pool", w.t[:, 0:kcn, :], dram_ap.rearrange("p (k m) -> p k m", m=128), writes=[w],
                  key="w4_%d" % (wc[0] % 8))
            wc[0] += 1
            return w

        sqr = Ring([sb(ph, "sq4_%d" % i, [128, NC], BF16) for i in range(3)])
        ss_main, ss_halo = banks[5], banks[6]

        def load_h_mo(it):
            jt = (NOWN + it * 512) // 256
            P.dma("sp", hs.t[:, :, 2:258], Hs[jt], writes=hsc, key="h4", par=True)
            P.dma("sp", hs.t[:, :, 258:NC], Hs[jt + 1], writes=hsc, key="h4", par=True)
            if it == 0:
                P.dma("sp", hs.t[:, :, 0:2], Hs[7][:, :, 254:256], writes=hsc, key="h4", par=True)
            P.dma("sp", mo.t[:, :, 2:NC], MOs[:, :, 2 + it * 512:2 + (it + 1) * 512].rearrange("c p t -> p c t"),
                  writes=[mo], key="mo4")
            if it == 0:
                P.dma("sp", mo.t[:, :, 0:2], MOs[:, :, 0:2].rearrange("c p t -> p c t"), writes=[mo], key="mo4")

        def load_x(it):
            jt = (NOWN + it * 512) // 256
            P.dma("sp", xs.t[:, :, 2:258], xT[jt], writes=[xs], key="x4")
            P.dma("sp", xs.t[:, :, 258:NC], xT[jt + 1], writes=[xs], key="x4")
            if it == 0:
                P.dma("sp", xs.t[:, :, 0:2], xT[7][:, :, 254:256], writes=[xs], key="x4")

        load_h_mo(0)
        load_x(0)
        for it in range(4):
            segs = ([(0, 2)] if it == 0 else []) + [(2, 512)]
            lo = segs[0][0]

            def linear(w, kcn, rhs_T, rhs_of_kc):
                outs = []
                for (c, n) in segs:
                    b = pb.next()
                    for kc in range(kcn):
                        rT = rhs_T[kc] if isinstance(rhs_T, list) else rhs_T
                        mm(b, b.t[:, 0:n], w.t[:, kc, :], rhs_of_kc(kc, c, n), kc == 0, kc == kcn - 1, [w, rT])
                    outs.append((b, c, n))
                return outs

            def sumsq_then(m, lagq, seglist):
                sq = sqr.next()
                c_lo = seglist[0][0]
                act(sq, sq.t[:, c_lo:NC], xs.t[:, m, c_lo:NC], AF.Square, [xs])

                def emit(m=m, sq=sq):
                    for (c, n) in seglist:
                        sbk = ss_halo if n == 2 else ss_main
                        mm(sbk, sbk.t[:, 0:n], ones.t[:], sq.t[:, c:c + n], m == 0, m == 15, [ones, sq])
                lagq.append(emit)
                while len(lagq) > 1:
                    lagq.pop(0)()

            def rstd_cols(rs, seglist):
                for (c, n) in seglist:
                    sbk = ss_halo if n == 2 else ss_main
                    act(rs, rs.t[:, c:c + n], sbk.t[:, 0:n], AF.Ln, [sbk, vecs], bias=vcol(V_EPS), scale=1.0 / D)
                    act(rs, rs.t[:, c:c + n], rs.t[:, c:c + n], AF.Exp, [rs], scale=-0.5)

            for m in range(16):
                wga = wload(w_gate_d[m], 16)
                woa = wload(w_omla_d[m], 8)
                wgb = wload(w_gate_d[16 + m], 16)
                wob = wload(w_odil_d[m], 4)
                tms = tmr.next()
                for half, (wg, wo, kcn, mobase) in enumerate(((wga, woa, 8, 0), (wgb, wob, 4, 8))):
                    sg = sgr.next()
                    for (b, c, n) in linear(wg, 16, hsc, lambda kc, c, n: hs.t[:, kc, c:c + n]):
                        act(sg, sg.t[:, c:c + n], b.t[:, 0:n], AF.Sigmoid, [b, vecs], bias=vcol(V_BGATE + half * 16 + m))
                    for (b, c, n) in linear(wo, kcn, mo, lambda kc, c, n, mb=mobase: mo.t[:, mb + kc, c:c + n]):
                        if half == 0:
                            tt(tms, tms.t[:, c:c + n], b.t[:, 0:n], sg.t[:, c:c + n], ALU.mult, [b, sg])
                        else:
                            tm2 = tmr.next()
                            tt(tm2, tm2.t[:, c:c + n], b.t[:, 0:n], sg.t[:, c:c + n], ALU.mult, [b, sg])
                            tt(sqm, sqm.t[:, m, c:c + n], tms.t[:, c:c + n], tm2.t[:, c:c + n], ALU.add, [tms, tm2])
            lagq = []
            for m in range(16):
                w = wload(w_out_d[m], 16)
                for (b, c, n) in linear(w, 16, sqm, lambda kc, c, n: sqm.t[:, kc, c:c + n]):
                    tt(xs, xs.t[:, m, c:c + n], b.t[:, 0:n], xs.t[:, m, c:c + n], ALU.add, [b, xs])
                sumsq_then(m, lagq, segs)
                act(hsc[m], hs.t[:, m, lo:NC], xs.t[:, m, lo:NC], AF.Copy, [xs, vecs], scale=vcol(V_FFN_G + m))
            while lagq:
                lagq.pop(0)()
            rs = rsr.next()
            rstd_cols(rs, segs)
            for kc in range(16):
                tt(hsc[kc], hs.t[:, kc, lo:NC], hs.t[:, kc, lo:NC], rs.t[:, lo:NC], ALU.mult, [hsc[kc], rs])
            lagq = []
            def ffn_chunk(ci, dst_T, dst_slot):
                accs = []
                for which, mo_i in ((0, ci), (1, 43 + ci)):
                    w = wload(w_up_d[mo_i], 16)
                    u0 = u0r.next()
                    if it > 0:
                        act(u0, u0.t[:, 0:2], carry.t[:, mo_i, :], AF.Copy, [carry])
                    for (b, c, n) in linear(w, 16, hsc, lambda kc, c, n: hs.t[:, kc, c:c + n]):
                        act(u0, u0.t[:, c:c + n], b.t[:, 0:n], AF.Copy, [b])
                    if it < 3:
                        act(carry, carry.t[:, mo_i, :], u0.t[:, 512:514], AF.Copy, [u0])
                    ac = acr.next()
                    cw = V_CONVW
                    act(ac, ac.t[:], u0.t[:, 2:514], AF.Identity, [u0, vecs], bias=vcol(V_CONVB + mo_i),
                        scale=vcol(cw + 2 * 86 + mo_i))
                    stt(ac, ac.t[:], u0.t[:, 1:513], vcol(cw + 1 * 86 + mo_i), ac.t[:], ALU.mult, ALU.add, [u0, ac, vecs])
                    stt(ac, ac.t[:], u0.t[:, 0:512], vcol(cw + 0 * 86 + mo_i), ac.t[:], ALU.mult, ALU.add, [u0, ac, vecs])
                    accs.append(ac)
                sl_ = tmr.next()
                act(sl_, sl_.t[:, 0:512], accs[1].t[:], AF.Silu, [accs[1]])
                tt(dst_T, dst_T.t[:, dst_slot, :], sl_.t[:, 0:512], accs[0].t[:], ALU.mult, [sl_, accs[0]])

            def w_down(hf, kcs, wdn, src_of_kc, last):
                for m in range(16):
                    wd = wdr.next()
                    P.dma("pool", wd.t[:, 0:len(kcs), :],
                          wdn[m].rearrange("p (k m) -> p k m", m=128)[:, kcs[0]:kcs[-1] + 1, :], writes=[wd],
                          key="wd4_%d" % (wc[1] % 3))
                    wc[1] += 1
                    b = pb.next()
                    for i, kc in enumerate(kcs):
                        src_T, src_ap = src_of_kc(kc)
                        mm(b, b.t[:, :], wd.t[:, i, :], src_ap, i == 0, i == len(kcs) - 1, [wd, src_T])
                    tt(xs, xs.t[:, m, 2:NC], b.t[:, :], xs.t[:, m, 2:NC], ALU.add, [b, xs])
                    if last:
                        sumsq_then(m, lagq, [(2, 512)])

            for ci in range(0, 22):
                ffn_chunk(ci, hid, ci)
            for ci in range(22, 24):
                ffn_chunk(ci, hidb, ci - 22)
            w_down(0, list(range(22)), w_dn0_d, lambda kc: (hid, hid.t[:, kc, :]), False)
            for ci in range(24, 43):
                ffn_chunk(ci, hid, ci - 24)
            if it < 3:
                load_h_mo(it + 1)
            src1 = lambda kc: (hidb, hidb.t[:, kc, :]) if kc < 2 else (hid, hid.t[:, kc - 2, :])
            w_down(1, list(range(0, 19)), w_dn1_d, src1, False)
            w_down(1, list(range(19, 21)), w_dn1_d, src1, True)
            while lagq:
                lagq.pop(0)()
            rs2 = rsr.next()
            rstd_cols(rs2, [(2, 512)])
            for kc in range(16):
                stt(xs, xs.t[:, kc, 2:NC], xs.t[:, kc, 2:NC], vcol(V_FIN_G + kc), rs2.t[:, 2:NC], ALU.mult, ALU.mult,
                    [xs, rs2, vecs])
            P.dma("sp", outT[it], xs.t[:, :, 2:NC], reads=[xs], key="out4")
            if it < 3:
                load_x(it + 1)
        P.end_phase()
    return nc, es, P


def _wtile(W):
    K, M = W.shape
    return np.ascontiguousarray(W.reshape(K // 128, 128, M // 128, 128).transpose(2, 1, 0, 3)).reshape(M // 128, 128, (K // 128) * 128)


def _pvec(v):
    return np.asarray(v, np.float32).reshape(-1, 128).T


def _constants():
    tri = (np.arange(128)[:, None] <= np.arange(128)[None, :]).astype(np.float32)
    ident = np.eye(128, dtype=np.float32)
    slopes = (2.0 ** (-8.0 * np.arange(1, 13, dtype=np.float32) / 12.0)).astype(np.float32)
    k = np.arange(128, dtype=np.float32)[:, None]
    q = np.arange(128, dtype=np.float32)[None, :]
    bm = np.zeros((128, 12, 2, 128), np.float32)
    for g in range(3):
        for s in range(4):
            hd = g * 4 + s
            sl = slopes[hd] * DILS[g]
            bm[:, hd, 0, :] = np.where(k <= q, -sl * (q - k), NEG)
            bm[:, hd, 1, :] = np.where(k >= q, -sl * (q + 128.0 - k), NEG)
    return tri, ident, bm.reshape(128, -1)


def _rope_tables(half):
    pos = np.arange(SEQ, dtype=np.float64) - (0.0 if half == 1 else float(NOWN))
    pos = np.maximum(pos, 0.0)
    inv_freq = 10000.0 ** (-np.arange(0, 64, 2, dtype=np.float64) / 64.0)
    ang = pos[None, :] * inv_freq[:, None]
    c, s = np.cos(ang).astype(np.float32), np.sin(ang).astype(np.float32)
    return np.concatenate([c, c], 0), np.concatenate([-s, s], 0)


_CACHE = {}


def kernel(x, attn_norm_g, w_in, b_gate, q_norm_g, w_uq, kv_norm_g, w_ukv, w_o_mla, w_o_dil, w_out, ffn_norm_g,
           w_up, conv_w, conv_b, w_down, final_norm_g):
    f = lambda a: np.asarray(a, np.float32)
    x, w_in, w_uq, w_ukv = f(x), f(w_in)[0], f(w_uq)[0], f(w_ukv)[0]
    swap = np.concatenate([np.arange(32, 64), np.arange(0, 32)])
    w_lat = _wtile(np.concatenate([w_in[:, 0:832], w_in[:, 768:832][:, swap]], 1))
    w_dil = _wtile(w_in[:, 832:5440])
    w_gate = _wtile(w_in[:, 5440:9536])
    uq = w_uq.reshape(512, 8, 192)
    uq_ext = np.concatenate([uq[:, :, 0:128], uq[:, :, 128:192], uq[:, :, 128:192][:, :, swap]], 2).reshape(512, 8 * 256)
    shared = {
        "w_lat": w_lat, "w_dil": w_dil, "w_gate": w_gate, "w_uq": _wtile(uq_ext), "w_ukv": _wtile(w_ukv),
        "w_omla": _wtile(f(w_o_mla)[0]), "w_odil": _wtile(f(w_o_dil)[0]), "w_out": _wtile(f(w_out)[0]),
        "w_up": _wtile(f(w_up)[0]), "w_dn0": _wtile(f(w_down)[0][0:2816]), "w_dn1": _wtile(f(w_down)[0][2816:]),
    }
    tri, ident, bm = _constants()
    shared.update({"tri": tri, "ident": ident, "biasmat": bm})
    vec_common = np.concatenate([
        _pvec(f(attn_norm_g)[0]), _pvec(f(ffn_norm_g)[0]), _pvec(f(final_norm_g)), _pvec(f(q_norm_g)[0]),
        _pvec(f(kv_norm_g)[0]), _pvec(f(b_gate)[0]), _pvec(f(conv_b)[0]),
        f(conv_w)[0].reshape(3, 86, 128).transpose(2, 0, 1).reshape(128, 258)], 1)
    in_maps = []
    for c in range(8):
        b, half = c // 2, c % 2
        if half == 1:
            xl = x[b]
        else:
            xl = np.concatenate([np.zeros((NOWN, D), np.float32), x[b, :NOWN]], 0)
        xt = np.ascontiguousarray(xl.reshape(16, 256, 16, 128).transpose(0, 3, 2, 1))
        extra = np.zeros((128, 3), np.float32)
        extra[:, 0] = 0.0 if half == 1 else NEG
        extra[:, 1] = EPS
        extra[:, 2] = 1e-18
        cosT, sinT = _rope_tables(half)
        m = dict(shared)
        m.update({"xT": xt, "vecs": np.ascontiguousarray(np.concatenate([vec_common, extra], 1)), "cosT": cosT, "sinT": sinT})
        in_maps.append(m)
    if _CACHE.get("maps_only"):
        return in_maps
    if "nc" not in _CACHE:
        _CACHE["nc"] = build_program()
    nc = _CACHE["nc"][0]
    res = run_bass_kernel_spmd(nc, in_maps, core_ids=list(range(8)))
    out = np.empty((4, SEQ, D), np.float32)
    for c in range(8):
        b, half = c // 2, c % 2
        o = res.results[c]["outT"]
        out[b, half * NOWN:(half + 1) * NOWN] = o.transpose(0, 3, 2, 1).reshape(NOWN, D)
    return out
```
